# Optimizing a Trainium2 kernel written in Bass

```python
import math
import jax, jax.numpy as jnp
from jax import lax
import numpy as np

D_MODEL = 1024
BATCH = 8
SEQ = 4096
DEPTH = 1

N_META = 16
ATTN_HEADS = 4
HEAD_DIM = 64
V_DIM = 2 * HEAD_DIM
ATTN_WIDTH = ATTN_HEADS * V_DIM
CONV_WIDTH = 512
CONV_GROUPS = 8
CONV_K = 3
N_BRANCH = 2
N_BUCKETS = 32
MAX_DISTANCE = 128
Q_BLOCK = 128
EPS = 1e-6
NEG_INF = -1e30

SPLIT_SIZES = (
    ATTN_HEADS * 2 * HEAD_DIM,
    ATTN_HEADS * 2 * HEAD_DIM,
    ATTN_WIDTH,
    ATTN_WIDTH,
    CONV_WIDTH,
    CONV_WIDTH,
    CONV_WIDTH,
    CONV_WIDTH,
    N_BRANCH * D_MODEL,
)
IN_COLS = int(sum(SPLIT_SIZES))

kernel_name = "hybrid_diffattn_shortconv_gated_merge"


def rms_norm(x, g):
    xf = x.astype(jnp.float32)
    r = xf * lax.rsqrt(jnp.mean(xf * xf, axis=-1, keepdims=True) + EPS)
    return (r * g.astype(jnp.float32)).astype(x.dtype)


def rel_bucket(n):
    max_exact = N_BUCKETS // 2
    nf = jnp.maximum(n, max_exact).astype(jnp.float32)
    large = max_exact + (jnp.log(nf / max_exact) / math.log(MAX_DISTANCE / max_exact)
                         * (N_BUCKETS - max_exact)).astype(jnp.int32)
    large = jnp.minimum(large, N_BUCKETS - 1)
    return jnp.where(n < max_exact, n, large)


def diff_attention(q, k, v, rel_bias, lam):
    B, L, H, _, Dh = q.shape
    nblk = L // Q_BLOCK
    scale = Dh ** -0.5
    qt = jnp.transpose(q.astype(jnp.float32), (0, 2, 3, 1, 4))
    kt = jnp.transpose(k.astype(jnp.float32), (0, 2, 3, 1, 4))
    vt = jnp.transpose(v.astype(jnp.float32), (0, 2, 1, 3))
    qb = qt.reshape(B, H, 2, nblk, Q_BLOCK, Dh)
    qb = jnp.moveaxis(qb, 3, 0)
    offsets = jnp.arange(nblk, dtype=jnp.int32) * Q_BLOCK
    kpos = jnp.arange(L, dtype=jnp.int32)
    bias_tab = rel_bias.astype(jnp.float32)

    def block(args):
        q_blk, q0 = args
        s = jnp.einsum('bhcqd,bhckd->bhcqk', q_blk, kt) * scale
        qpos = q0 + jnp.arange(Q_BLOCK, dtype=jnp.int32)
        dist = qpos[:, None] - kpos[None, :]
        bias = bias_tab[rel_bucket(jnp.maximum(dist, 0))]
        s = s + jnp.transpose(bias, (2, 3, 0, 1))[None]
        s = jnp.where((dist >= 0)[None, None, None], s, NEG_INF)
        p = jax.nn.softmax(s, axis=-1)
        a = p[:, :, 0] - lam * p[:, :, 1]
        return jnp.einsum('bhqk,bhkv->bhqv', a, vt)

    out = lax.map(block, (qb, offsets))
    out = jnp.transpose(out, (1, 0, 3, 2, 4)).reshape(B, L, H, -1)
    return out.astype(v.dtype)


def short_conv(u, w):
    L = u.shape[1]
    up = jnp.pad(u, ((0, 0), (CONV_K - 1, 0), (0, 0)))
    y = w[0] * up[:, 0:L]
    for j in range(1, CONV_K):
        y = y + w[j] * up[:, j:j + L]
    return y


def setup_inputs(seed: int = 0) -> dict:
    key = jax.random.key(seed)
    ks = jax.random.split(key, 16)
    f32 = jnp.float32
    x = jax.random.normal(ks[0], (BATCH, SEQ, D_MODEL), f32)
    meta_tokens = jax.random.normal(ks[1], (N_META, D_MODEL), f32)
    rel_bias = 0.5 * jax.random.normal(ks[2], (N_BUCKETS, ATTN_HEADS, 2), f32)
    norm_g = 1.0 + 0.02 * jax.random.normal(ks[3], (DEPTH, D_MODEL), f32)
    w_in = jax.random.normal(ks[4], (DEPTH, D_MODEL, IN_COLS), f32) * D_MODEL ** -0.5
    q_norm_g = 1.0 + 0.02 * jax.random.normal(ks[5], (DEPTH, HEAD_DIM), f32)
    k_norm_g = 1.0 + 0.02 * jax.random.normal(ks[6], (DEPTH, HEAD_DIM), f32)
    lambda_q1 = 0.1 * jax.random.normal(ks[7], (DEPTH, HEAD_DIM), f32)
    lambda_k1 = 0.1 * jax.random.normal(ks[8], (DEPTH, HEAD_DIM), f32)
    lambda_q2 = 0.1 * jax.random.normal(ks[9], (DEPTH, HEAD_DIM), f32)
    lambda_k2 = 0.1 * jax.random.normal(ks[10], (DEPTH, HEAD_DIM), f32)
    subln_g = 1.0 + 0.02 * jax.random.normal(ks[11], (DEPTH, V_DIM), f32)
    conv_w = jax.random.normal(ks[12], (DEPTH, CONV_K, CONV_WIDTH), f32) * CONV_K ** -0.5
    w_branch = jax.random.normal(ks[13], (DEPTH, N_BRANCH, ATTN_WIDTH, D_MODEL), f32) * ATTN_WIDTH ** -0.5
    w_out = jax.random.normal(ks[14], (DEPTH, D_MODEL, D_MODEL), f32) * D_MODEL ** -0.5
    return {"x": x, "meta_tokens": meta_tokens, "rel_bias": rel_bias, "norm_g": norm_g,
            "w_in": w_in, "q_norm_g": q_norm_g, "k_norm_g": k_norm_g,
            "lambda_q1": lambda_q1, "lambda_k1": lambda_k1, "lambda_q2": lambda_q2,
            "lambda_k2": lambda_k2, "subln_g": subln_g, "conv_w": conv_w,
            "w_branch": w_branch, "w_out": w_out}


def reference(x, meta_tokens, rel_bias, norm_g, w_in, q_norm_g, k_norm_g,
              lambda_q1, lambda_k1, lambda_q2, lambda_k2, subln_g, conv_w,
              w_branch, w_out):
    B, S, D = x.shape
    L = N_META + S
    Lp = ((L + Q_BLOCK - 1) // Q_BLOCK) * Q_BLOCK
    meta = jnp.broadcast_to(meta_tokens.astype(x.dtype)[None], (B, N_META, D))
    h = jnp.concatenate([meta, x], axis=1)
    h = jnp.pad(h, ((0, 0), (0, Lp - L), (0, 0)))
    split_idx = [int(i) for i in np.cumsum(SPLIT_SIZES)[:-1]]

    for layer in range(DEPTH):
        lam_init = 0.8 - 0.6 * math.exp(-0.3 * layer)
        xn = rms_norm(h, norm_g[layer])
        proj = jnp.einsum('bld,dc->blc', xn, w_in[layer])
        q, k, v, g_attn, c_b, c_c, c_h, g_conv, g_merge = jnp.split(proj, split_idx, axis=-1)

        q = rms_norm(q.reshape(B, Lp, ATTN_HEADS, 2, HEAD_DIM), q_norm_g[layer])
        k = rms_norm(k.reshape(B, Lp, ATTN_HEADS, 2, HEAD_DIM), k_norm_g[layer])
        v = v.reshape(B, Lp, ATTN_HEADS, V_DIM)
        lam = (jnp.exp(jnp.sum(lambda_q1[layer].astype(jnp.float32) * lambda_k1[layer].astype(jnp.float32)))
               - jnp.exp(jnp.sum(lambda_q2[layer].astype(jnp.float32) * lambda_k2[layer].astype(jnp.float32)))
               + lam_init)
        a = diff_attention(q, k, v, rel_bias, lam)
        a = rms_norm(a, subln_g[layer]) * (1.0 - lam_init)
        a = a.reshape(B, Lp, ATTN_WIDTH) * jax.nn.silu(g_attn)

        c = c_b * short_conv(c_c * c_h, conv_w[layer].astype(h.dtype))
        c = c * jax.nn.silu(g_conv)

        br = jnp.stack([a, c], axis=2)
        y = jnp.einsum('blnc,ncd->blnd', br, w_branch[layer])
        gate = jax.nn.sigmoid(g_merge.reshape(B, Lp, N_BRANCH, D))
        merged = jnp.sum(gate * y, axis=2)
        h = h + jnp.einsum('bld,de->ble', merged, w_out[layer])

    return h[:, N_META:N_META + S]
```

```python
import math
from contextlib import ExitStack

import numpy as np
import concourse.bass as bass
import concourse.mybir as mybir
from concourse.bass_utils import run_bass_kernel_spmd

F32 = mybir.dt.float32
BF16 = mybir.dt.bfloat16
AF = mybir.ActivationFunctionType
ALU = mybir.AluOpType
AX = mybir.AxisListType

S = 4096
D = 1024
NMETA = 16
NCORES = 8
CH1 = 512
CH2 = 256
NG = 272
GV = 400
EPS = 1e-6
LAM_INIT = 0.8 - 0.6 * math.exp(-0.3 * 0)
QK_SCALE = 64 ** -0.5
ARENA_BYTES = 212800


def _rel_bucket(n):
    n = np.asarray(n, dtype=np.int32)
    max_exact = 16
    nf = np.maximum(n, max_exact).astype(np.float32)
    large = max_exact + (np.log(nf / np.float32(max_exact)) / np.float32(math.log(128 / 16))
                         * np.float32(32 - max_exact)).astype(np.int32)
    large = np.minimum(large, 31)
    return np.where(n < max_exact, n, large)


class DSem:
    def __init__(self, handle):
        self.h = handle
        self.count = 0


class Plan:
    ENGS = ("pe", "act", "dve", "pool", "sp")

    def __init__(self, nc, es):
        self.nc = nc
        self.es = es
        self.ops = {e: [] for e in self.ENGS}
        self.sem = {e: es.enter_context(nc.semaphore("s_" + e)) for e in ("pe", "act", "dve", "pool")}
        self.cnt = {e: 0 for e in self.sem}
        self.last = {e: None for e in self.sem}
        self.pending = {e: [] for e in self.ENGS}

    def dsem(self, name):
        return DSem(self.es.enter_context(self.nc.semaphore(name)))

    def op(self, eng, fn, waits=(), after_prev=False):
        w = [t for t in waits if t is not None] + self.pending[eng]
        if after_prev and self.last[eng] is not None:
            w.append(self.last[eng])
        self.pending[eng] = []
        self.cnt[eng] += 1
        tok = (eng, self.sem[eng], self.cnt[eng])
        self.ops[eng].append((fn, w, tok, 1))
        self.last[eng] = tok
        return tok

    def dma(self, eng, fn, dsem, waits=()):
        w = [t for t in waits if t is not None] + self.pending[eng]
        self.pending[eng] = []
        dsem.count += 16
        tok = ("dma", dsem.h, dsem.count)
        self.ops[eng].append((fn, w, tok, 16))
        return tok

    def wait_only(self, eng, waits):
        self.ops[eng].append((None, [t for t in waits if t is not None], None, 0))

    def barrier(self):
        toks = [t for t in self.last.values() if t is not None]
        for e in self.ENGS:
            self.pending[e] = list(toks)

    def replay(self, eng, e):
        waited = {}
        for fn, waits, tok, inc in self.ops[eng]:
            for (src, s, v) in waits:
                if src == eng and eng == "pe":
                    continue
                k = id(s)
                if waited.get(k, 0) < v:
                    e.wait_ge(s, v)
                    waited[k] = v
            if fn is None:
                continue
            ins = fn(e)
            if tok is not None:
                ins.then_inc(tok[1], inc)


class Arena:
    def __init__(self, ar, nbytes):
        self.ar = ar
        self.n = nbytes
        self.top = 0

    def alloc(self, shape, dtype):
        esz = 2 if dtype == BF16 else 4
        per = int(np.prod(shape[1:])) * esz
        off = (self.top + 31) // 32 * 32
        per4 = (per + 3) // 4 * 4
        assert off + per4 <= self.n, f"arena overflow: {off + per4} > {self.n}"
        self.top = off + per4
        ap = self.ar[:, off // 4:(off + per4) // 4]
        if dtype == BF16:
            ap = ap.bitcast(BF16)
            ap = ap[:, 0:per // 2]
        if len(shape) == 3:
            ap = ap.rearrange("p (a b) -> p a b", a=shape[1])
        elif len(shape) == 4:
            ap = ap.rearrange("p (a b c) -> p a b c", a=shape[1], b=shape[2])
        if shape[0] < 128:
            ap = ap[0:shape[0]]
        return ap


def bcast_rows(dram_ap, nparts, ncols, off=0):
    return bass.AP(dram_ap.tensor, off, [[0, nparts], [1, ncols]])


def build_program(debug=False):
    nc = bass.Bass("TRN2", target_bir_lowering=False)
    dt_in = lambda name, shape: nc.dram_tensor(name, shape, F32, kind="ExternalInput").ap()
    x = dt_in("x", [S, D])
    meta = dt_in("meta", [NMETA, D])
    relb = dt_in("relb", [32, 8])
    norm_g = dt_in("norm_g", [1, D])
    w_in = dt_in("w_in", [D, 6144])
    qg = dt_in("qg", [1, 64])
    kg = dt_in("kg", [1, 64])
    lq1 = dt_in("lq1", [1, 64])
    lk1 = dt_in("lk1", [1, 64])
    lq2 = dt_in("lq2", [1, 64])
    lk2 = dt_in("lk2", [1, 64])
    subg = dt_in("subg", [1, 128])
    convw = dt_in("convw", [3, 512])
    wb = dt_in("wb", [1024, 1024])
    wo = dt_in("wo", [1024, 1024])
    oh = dt_in("oh", [32, NG])
    ident_d = dt_in("ident", [128, 128])
    bones_d = dt_in("bones", [128, 128])
    out = nc.dram_tensor("out", [S, D], F32, kind="ExternalOutput").ap()
    gscr = nc.dram_tensor("gscr", [8, GV], F32).ap()
    if debug:
        dbg_kt = nc.dram_tensor("dbg_kt", [128, 4 * S], F32, kind="ExternalOutput").ap()
        dbg_at = nc.dram_tensor("dbg_at", [128, 4 * S], F32, kind="ExternalOutput").ap()
        dbg_va = nc.dram_tensor("dbg_va", [128, 33 * 516], F32, kind="ExternalOutput").ap()
        dbg_mt = nc.dram_tensor("dbg_mt", [128, 2048], F32, kind="ExternalOutput").ap()
        dbg_mtm = nc.dram_tensor("dbg_mtm", [NMETA, 1024], F32, kind="ExternalOutput").ap()
        dbg_qt = nc.dram_tensor("dbg_qt", [128, 4 * CH1], F32, kind="ExternalOutput").ap()
        dbg_sm = nc.dram_tensor("dbg_sm", [128, 96], F32, kind="ExternalOutput").ap()
        dbg_ktm = nc.dram_tensor("dbg_ktm", [128, 64], F32, kind="ExternalOutput").ap()
        dbg_carry = nc.dram_tensor("dbg_carry", [128, 8], F32, kind="ExternalOutput").ap()
        dbg_xntm = nc.dram_tensor("dbg_xntm", [128, 128], F32, kind="ExternalOutput").ap()
    gscr2 = nc.dram_tensor("gscr2", [8, 128, GV], F32).ap()

    es = ExitStack()
    with es:
        arena_t = es.enter_context(nc.sbuf_tensor("arena", [128, ARENA_BYTES // 4], F32))
        psum = es.enter_context(nc.psum_tensor("psum", [128, 8, 512], F32))
        P = Plan(nc, es)
        A = Arena(arena_t, ARENA_BYTES)

        AT = A.alloc([128, 4, S], BF16)
        big = A.alloc([128, 53248], BF16)
        xr = A.alloc([128, 2, D], F32)
        xn = A.alloc([128, 4, D], BF16)
        xnT = A.alloc([128, 8, CH1], BF16)
        gb = A.alloc([128, D], F32)
        gqk = A.alloc([128, 2], F32)
        gsub = A.alloc([128, 128], F32)
        lamv = A.alloc([128, 4, 64], F32)
        lam_s = A.alloc([128, 8], F32)
        cw = A.alloc([128, 3, 4], F32)
        ident = A.alloc([128, 128], BF16)
        bones = A.alloc([128, 128], BF16)
        xnTm = A.alloc([128, 8, NMETA], BF16)
        rstd_all = A.alloc([128, 40], F32)
        ssq = A.alloc([128, 40], F32)
        frame = A.top

        W1 = big[:, 0:12288].rearrange("p (a b) -> p a b", a=8)
        KT = big[:, 12288:28672].rearrange("p (a b) -> p a b", a=4)
        VA = big[:, 28672:45700].rearrange("p (t h n) -> p t h n", t=33, h=4)
        KTm = big[:, 45700:45764].rearrange("p (a b) -> p a b", a=4)
        W2 = big[:, 0:36864].rearrange("p (a b) -> p a b", a=8)
        WB = big[:, 36864:45056].rearrange("p (n f c) -> p n f c", n=2, f=4)
        WO = big[:, 45056:53248].rearrange("p (a b) -> p a b", a=8)

        QT = A.alloc([128, 4, CH1], BF16)
        sq = [A.alloc([128, CH1], BF16) for _ in range(2)]
        lnr = [A.alloc([128, CH1], F32) for _ in range(2)]
        et_off = A.top
        et = [A.alloc([128, 2, CH1], BF16) for _ in range(4)]
        attn = A.alloc([128, 4, 512], BF16)
        setup_end = A.top
        MT = A.alloc([128, 8, 256], F32)
        MTm = A.alloc([NMETA, 8, 128], F32)
        att = A.alloc([128, 4, 128], F32)
        accs = A.alloc([128, 4, 258], F32)
        rz = A.alloc([128, 4, 2], F32)
        ssum = A.alloc([128, 4], F32)
        rs = A.alloc([128, 4], F32)
        junk = A.alloc([128, 64], F32)
        s1_top = A.top
        A.top = et_off
        xm = A.alloc([NMETA, D], F32)
        xnm = A.alloc([NMETA, D], BF16)
        gv = A.alloc([8, GV], F32)
        gd = A.alloc([8, NG], F32)
        rb = A.alloc([32, 8], F32)
        ohs = A.alloc([32, NG], F32)
        assert A.top <= setup_end, (A.top, setup_end)
        A.top = s1_top

        A.top = frame
        sg = [A.alloc([128, CH2], F32) for _ in range(2)]
        chb = [A.alloc([128, CH2], F32) for _ in range(2)]
        ubuf = [A.alloc([128, CH2 + 2], F32) for _ in range(2)]
        yb = [A.alloc([128, CH2], F32) for _ in range(2)]
        sg2 = [A.alloc([128, CH2], F32) for _ in range(2)]
        g0b = [A.alloc([128, CH2], F32) for _ in range(5)]
        g1b = [A.alloc([128, CH2], F32) for _ in range(5)]
        cbs = [A.alloc([128, CH2], F32) for _ in range(2)]
        cT = A.alloc([128, 4, CH2], BF16)
        aTg = A.alloc([128, 4, CH2], BF16)
        mT = A.alloc([128, 8, CH2], BF16)
        ob = [A.alloc([128, D], F32) for _ in range(2)]
        carry = A.alloc([128, 4, 2], F32)
        chm4 = A.alloc([128, 4, NMETA], F32)
        um4 = A.alloc([128, 4, NMETA], F32)

        pbf = psum[:, :, :].rearrange("p a b -> p (a b)").bitcast(BF16).rearrange("p (a b) -> p a b", a=8)

        s_const = P.dsem("d_const")
        s_w1 = [P.dsem(f"d_w1_{i}") for i in range(4)]
        s_w2 = [P.dsem(f"d_w2_{i}") for i in range(4)]
        s_gs = P.dsem("d_gs")
        s_mt = P.dsem("d_mt")
        s_xr = [P.dsem(f"d_xr{i}") for i in range(2)]
        s_obl = [P.dsem(f"d_obl{i}") for i in range(2)]
        s_obs = [P.dsem(f"d_obs{i}") for i in range(2)]

        bank_free = [None] * 8
        ring_pos = [0]

        def bank_acquire():
            i = ring_pos[0] % 8
            ring_pos[0] += 1
            return i, bank_free[i]

        ctoks = []
        def cdma(out_ap, in_ap, slow=False):
            kw = {"allow_slow_non_contiguous": True} if slow else {}
            ctoks.append(P.dma("sp", lambda e, o=out_ap, i=in_ap, kw=kw: e.dma_start(out=o, in_=i, **kw), s_const))
        cdma(gb, bcast_rows(norm_g, 128, D))
        cdma(xm, meta[:, :])
        cdma(gqk[0:64, 0:1], bass.AP(qg.tensor, 0, [[1, 64], [1, 1]]))
        cdma(gqk[64:128, 0:1], bass.AP(qg.tensor, 0, [[1, 64], [1, 1]]))
        cdma(gqk[0:64, 1:2], bass.AP(kg.tensor, 0, [[1, 64], [1, 1]]))
        cdma(gqk[64:128, 1:2], bass.AP(kg.tensor, 0, [[1, 64], [1, 1]]))
        cdma(gsub, bcast_rows(subg, 128, 128))
        for i, v in enumerate((lq1, lk1, lq2, lk2)):
            cdma(lamv[:, i, :], bcast_rows(v, 128, 64))
        for j_ in range(3):
            for fc_ in range(4):
                cdma(cw[:, j_, fc_:fc_ + 1], bass.AP(convw.tensor, j_ * 512 + fc_ * 128, [[1, 128], [1, 1]]))
        cdma(rb, relb[:, :])
        cdma(ohs, oh[:, :])
        const_tok = ctoks[-1]

        w1toks = []
        def wdma(toks, sems, fn, extra=()):
            i = len(toks)
            toks.append(P.dma("pool", fn, sems[i % 4], waits=([toks[i - 4]] if i >= 4 else []) + list(extra)))
        wdma(w1toks, s_w1, lambda e: e.dma_start(out=ident, in_=ident_d[:, :]))
        wdma(w1toks, s_w1, lambda e: e.dma_start(out=bones, in_=bones_d[:, :]))
        w1_grp = {}
        for gname, c0 in (("k", 512), ("q", 0), ("v", 1024)):
            for dc in range(8):
                wdma(w1toks, s_w1, lambda e, dc=dc, c0=c0: e.dma_start(out=W1[:, dc, c0:c0 + 512],
                                                                     in_=w_in[dc * 128:(dc + 1) * 128, c0:c0 + 512]))
            w1_grp[gname] = w1toks[-4:]
        w1_id = w1toks[0:2]
        wotoks = []
        s_wo = [P.dsem(f"d_wo_{i}") for i in range(4)]

        P.op("pool", lambda e: e.memset(VA[:, :, :, 128:129], 1.0))
        P.op("pool", lambda e: e.memset(gv, 0.0))

        t = P.op("dve", lambda e: e.scalar_tensor_tensor(out=junk, in0=lamv[:, 0, :], scalar=1.0, in1=lamv[:, 1, :],
                                                         op0=ALU.mult, op1=ALU.mult, accum_out=lam_s[:, 0:1]),
                 waits=[const_tok])
        t = P.op("dve", lambda e: e.scalar_tensor_tensor(out=junk, in0=lamv[:, 2, :], scalar=1.0, in1=lamv[:, 3, :],
                                                         op0=ALU.mult, op1=ALU.mult, accum_out=lam_s[:, 1:2]))
        t = P.op("act", lambda e: e.activation(out=lam_s[:, 2:4], in_=lam_s[:, 0:2], func=AF.Exp), waits=[t])
        t = P.op("dve", lambda e: e.tensor_tensor(out=lam_s[:, 4:5], in0=lam_s[:, 2:3], in1=lam_s[:, 3:4], op=ALU.subtract), waits=[t])
        t = P.op("dve", lambda e: e.tensor_scalar(out=lam_s[:, 5:6], in0=lam_s[:, 4:5], scalar1=LAM_INIT, scalar2=-1.0,
                                                  op0=ALU.add, op1=ALU.mult), after_prev=True)
        neglam = lam_s[:, 5:6]
        P.op("dve", lambda e: e.tensor_scalar(out=gsub, in0=gsub, scalar1=1.0 - LAM_INIT, scalar2=None, op0=ALU.mult))


        xr_free = [None, None]
        xn_free = [None]
        xn_ready = {}

        def x_load(n):
            slot = n % 2
            tk = P.dma("sp", lambda e, n=n, slot=slot: e.dma_start(out=xr[:, slot, :], in_=x[n * 128:(n + 1) * 128, :]),
                       s_xr[slot], waits=[xr_free[slot]])
            return tk

        def x_sumsq(n, ldtok, xslot):
            slot = n % 2
            return P.op("dve", lambda e, n=n, slot=slot, xslot=xslot: e.scalar_tensor_tensor(
                out=xn[:, xslot, :], in0=xr[:, slot, :], scalar=1.0, in1=xr[:, slot, :],
                op0=ALU.mult, op1=ALU.mult, accum_out=ssq[:, n:n + 1]), waits=[ldtok, xn_free[0]])

        def x_rstd(n, tk):
            t1 = P.op("act", lambda e, n=n: e.activation(out=rstd_all[:, n:n + 1], in_=ssq[:, n:n + 1], func=AF.Ln,
                                                         scale=1.0 / D, bias=EPS), waits=[tk])
            return P.op("act", lambda e, n=n: e.activation(out=rstd_all[:, n:n + 1], in_=rstd_all[:, n:n + 1], func=AF.Exp,
                                                           scale=-0.5), after_prev=True)

        def x_norm(n, xslot, waits):
            slot = n % 2
            tk = P.op("dve", lambda e, n=n, slot=slot, xslot=xslot: e.scalar_tensor_tensor(
                out=xn[:, xslot, :], in0=xr[:, slot, :], scalar=rstd_all[:, n:n + 1], in1=gb,
                op0=ALU.mult, op1=ALU.mult), waits=waits)
            xr_free[slot] = tk
            xn_ready[n] = tk
            return tk

        t = P.op("dve", lambda e: e.scalar_tensor_tensor(out=xnm, in0=xm, scalar=1.0, in1=xm, op0=ALU.mult,
                                                         op1=ALU.mult, accum_out=ssq[0:NMETA, 32:33]), waits=[const_tok])
        t = P.op("act", lambda e: e.activation(out=rstd_all[0:NMETA, 32:33], in_=ssq[0:NMETA, 32:33], func=AF.Ln,
                                               scale=1.0 / D, bias=EPS), waits=[t])
        t = P.op("act", lambda e: e.activation(out=rstd_all[0:NMETA, 32:33], in_=rstd_all[0:NMETA, 32:33], func=AF.Exp, scale=-0.5), after_prev=True)
        t = P.op("dve", lambda e: e.scalar_tensor_tensor(out=xnm, in0=xm, scalar=rstd_all[0:NMETA, 32:33], in1=gb[0:NMETA, :],
                                                         op0=ALU.mult, op1=ALU.mult), waits=[t])
        bi, bw = bank_acquire()
        def f_mt(e, bi=bi):
            ins = None
            for dc in range(8):
                ins = e.transpose(out=pbf[:, bi, dc * NMETA:(dc + 1) * NMETA], in_=xnm[:, dc * 128:(dc + 1) * 128],
                                  identity=ident[0:NMETA, 0:NMETA])
            return ins
        t = P.op("pe", f_mt, waits=[t, bw] + w1_id)
        t = P.op("dve", lambda e, bi=bi: e.tensor_copy(out=xnTm, in_=pbf[:, bi, 0:8 * NMETA].rearrange("p (a b) -> p a b", a=8)),
                 waits=[t])
        bank_free[bi] = t
        xnTm_tok = t

        def do_transposes(ntiles, tile0, xnT_dst):
            last = None
            evs = []
            for s_ in range(ntiles):
                bi, bw = bank_acquire()
                def f(e, bi=bi, s_=s_):
                    ins = None
                    for dc in range(8):
                        ins = e.transpose(out=pbf[:, bi, dc * 128:(dc + 1) * 128], in_=xn[:, s_, dc * 128:(dc + 1) * 128],
                                          identity=ident)
                    return ins
                tp = P.op("pe", f, waits=[xn_ready[tile0 + s_], bw])
                src = pbf[:, bi, :].rearrange("p (a b) -> p a b", a=8)
                dst = xnT_dst[:, :, s_ * 128:(s_ + 1) * 128]
                tc_ = P.op("act", lambda e, src=src, dst=dst: e.activation(out=dst, in_=src, func=AF.Copy), waits=[tp])
                bank_free[bi] = tc_
                evs.append(tc_)
                last = tp
            xn_free[0] = last
            xnT_tok[0] = evs
            return last

        sqi = [0]
        xnT_tok = [[xnTm_tok]]
        rfree = {}

        def qk_units(units):
            st = []
            for u in range(len(units) + 1):
                if u < len(units):
                    woff, gcol, src, n, dest = units[u]
                    bi, bw = bank_acquire()
                    def f(e, bi=bi, woff=woff, src=src, n=n):
                        ins = None
                        for dc in range(8):
                            ins = e.matmul(psum[:, bi, 0:n], lhsT=W1[:, dc, woff:woff + 128], rhs=src[:, dc, 0:n],
                                           start=(dc == 0), stop=(dc == 7))
                        return ins
                    tp = P.op("pe", f, waits=[bw] + w1_grp["k" if woff >= 512 else "q"] + xnT_tok[0])
                    k = sqi[0] % 2
                    sqi[0] += 1
                    tsq = P.op("act", lambda e, bi=bi, n=n, k=k: e.activation(out=sq[k][:, 0:n], in_=psum[:, bi, 0:n], func=AF.Square),
                               waits=[tp])
                    st.append((bi, k, tsq, gcol, n, dest))
                if u >= 1:
                    bi, k, tsq, gcol, n, dest = st[u - 1]
                    b2, bw2 = bank_acquire()
                    tss = P.op("pe", lambda e, b2=b2, k=k, n=n: e.matmul(psum[:, b2, 0:n], lhsT=bones, rhs=sq[k][:, 0:n],
                                                                          start=True, stop=True), waits=[tsq, bw2])
                    P.op("act", lambda e, b2=b2, k=k, n=n: e.activation(out=lnr[k][:, 0:n], in_=psum[:, b2, 0:n], func=AF.Ln,
                                                                       scale=1.0 / 64, bias=EPS), waits=[tss, rfree.get(("lnr", k))])
                    tr = P.op("act", lambda e, k=k, n=n: e.activation(out=lnr[k][:, 0:n], in_=lnr[k][:, 0:n], func=AF.Exp, scale=-0.5), after_prev=True)
                    bank_free[b2] = tr
                    td = P.op("dve", lambda e, bi=bi, k=k, n=n, gcol=gcol, dest=dest: e.scalar_tensor_tensor(
                        out=dest, in0=psum[:, bi, 0:n], scalar=gqk[:, gcol:gcol + 1], in1=lnr[k][:, 0:n],
                        op0=ALU.mult, op1=ALU.mult), waits=[tr])
                    bank_free[bi] = td
                    rfree[("lnr", k)] = td

        def v_proj(src, ntok, s_off, vtile):
            bi, bw = bank_acquire()
            def f(e, bi=bi):
                ins = None
                for dc in range(8):
                    ins = e.matmul(psum[0:ntok, bi, :], lhsT=src[:, dc, s_off:s_off + ntok], rhs=W1[:, dc, 1024:1536],
                                   start=(dc == 0), stop=(dc == 7))
                return ins
            tp = P.op("pe", f, waits=[bw] + w1_grp["v"] + xnT_tok[0])
            tv = P.op("dve", lambda e, bi=bi: e.tensor_copy(out=VA[0:ntok, vtile, :, 0:128],
                                                           in_=psum[0:ntok, bi, :].rearrange("p (h n) -> p h n", h=4)), waits=[tp])
            bank_free[bi] = tv

        def at_transposes(tok0):
            for jb in range(4):
                bi, bw = bank_acquire()
                def f(e, bi=bi, jb=jb):
                    ins = None
                    for h in range(4):
                        ins = e.transpose(out=pbf[:, bi, h * 128:(h + 1) * 128], in_=attn[:, jb, h * 128:(h + 1) * 128], identity=ident)
                    return ins
                tp = P.op("pe", f, waits=[bw])
                src = pbf[:, bi, 0:512].rearrange("p (a b) -> p a b", a=4)
                dst = AT[:, :, tok0 + jb * 128: tok0 + (jb + 1) * 128]
                tc_ = P.op("act", lambda e, src=src, dst=dst: e.activation(out=dst, in_=src, func=AF.Copy), waits=[tp])
                bank_free[bi] = tc_

        qk_units([(512 + h * 128, 1, xnTm, NMETA, KTm[:, h, :]) for h in range(4)])

        for n in range(4):
            ld = x_load(n)
            tk = x_sumsq(n, ld, n)
            tk = x_rstd(n, tk)
            x_norm(n, n, [tk])

        bi, bw = bank_acquire()
        t = P.op("pe", lambda e, bi=bi: e.matmul(psum[0:8, bi, 0:NG], lhsT=rb[:, :], rhs=ohs[:, :], start=True, stop=True),
                 waits=[const_tok])
        t = P.op("dve", lambda e, bi=bi: e.tensor_scalar(out=gd, in0=psum[0:8, bi, 0:NG], scalar1=psum[0:8, bi, NG - 1:NG],
                                                  scalar2=None, op0=ALU.subtract), waits=[t])
        bank_free[bi] = t
        t = P.op("act", lambda e: e.activation(out=gv[:, 127:127 + NG], in_=gd, func=AF.Exp), waits=[t, P.last["pool"]])
        t = P.dma("sp", lambda e: e.dma_start(out=gscr[:, :], in_=gv), s_gs, waits=[t])
        t = P.dma("sp", lambda e: e.dma_start(out=gscr2[:, :, :], in_=bass.AP(gscr.tensor, 0, [[GV, 8], [0, 128], [1, GV]])),
                  s_gs, waits=[t])
        mtoks = []
        for hc in range(8):
            mtoks.append(P.dma("sp", lambda e, hc=hc: e.dma_start(
                out=MT[:, hc, :], in_=bass.AP(gscr2.tensor, hc * 128 * GV + 127, [[GV - 1, 128], [1, 256]])), s_mt, waits=[t]))
            mtoks.append(P.dma("sp", lambda e, hc=hc: e.dma_start(
                out=MTm[:, hc, :], in_=bass.AP(gscr2.tensor, hc * 128 * GV + 127 + NMETA, [[GV - 1, NMETA], [1, 128]])), s_mt, waits=[t]))
        mt_tok = mtoks[-1]

        acc = psum[:, 4:8, :]
        accv = acc[:, :, 0:258].rearrange("p j (c n) -> p j c n", c=2)
        st_free = [None, None]
        et_free = [None] * 4
        eti = [0]
        acc_free = [None]
        NCH1 = S // CH1
        W2_EARLY = [(0, 0), (0, 1), (0, 2), (1, 0), (1, 1), (1, 2), (2, 0), (2, 1)]
        w2toks = []

        accsv = accs.rearrange("p j (c n) -> p j c n", c=2)

        def head_post_evac(last_av):
            t0 = P.op("dve", lambda e: e.tensor_copy(out=accs, in_=acc[:, :, 0:258]), waits=[last_av])
            acc_free[0] = t0
            P.op("dve", lambda e: e.reciprocal(out=rz, in_=accsv[:, :, :, 128]), after_prev=True)
            P.op("dve", lambda e: e.tensor_scalar(out=rz[:, :, 1], in0=rz[:, :, 1], scalar1=neglam, scalar2=None, op0=ALU.mult),
                 after_prev=True)
            for jb in range(4):
                P.op("dve", lambda e, jb=jb: e.tensor_scalar(out=accsv[:, jb, 1, 0:128], in0=accsv[:, jb, 1, 0:128], scalar1=rz[:, jb, 1:2],
                                                            scalar2=None, op0=ALU.mult), after_prev=(jb == 0))
            for jb in range(4):
                P.op("dve", lambda e, jb=jb: e.scalar_tensor_tensor(out=att[:, jb, :], in0=accsv[:, jb, 0, 0:128],
                                                                    scalar=rz[:, jb, 0:1], in1=accsv[:, jb, 1, 0:128],
                                                                    op0=ALU.mult, op1=ALU.add), after_prev=(jb == 0))
            P.op("dve", lambda e: e.tensor_tensor(out=accsv[:, :, 1, 0:128], in0=att, in1=att, op=ALU.mult), after_prev=True)
            return P.op("dve", lambda e: e.tensor_reduce(out=ssum, in_=accsv[:, :, 1, 0:128], axis=AX.X, op=ALU.add), after_prev=True)

        def head_post_act(tk):
            P.op("act", lambda e: e.activation(out=rs, in_=ssum, func=AF.Ln, scale=1.0 / 128, bias=EPS), waits=[tk])
            return P.op("act", lambda e: e.activation(out=rs, in_=rs, func=AF.Exp, scale=-0.5), after_prev=True)

        def head_post_fin(h, tk):
            tl = None
            for jb in range(4):
                tl = P.op("dve", lambda e, jb=jb, h=h: e.scalar_tensor_tensor(
                    out=attn[:, jb, h * 128:(h + 1) * 128], in0=att[:, jb, :], scalar=rs[:, jb:jb + 1], in1=gsub,
                    op0=ALU.mult, op1=ALU.mult), waits=[tk] if jb == 0 else [])
            return tl

        attn_ready = [None]
        last_exp = [None]
        last_evac = [None]
        posts = []

        def flush_posts():
            while posts:
                hp, tkp = posts.pop(0)
                tk2 = head_post_act(tkp)
                attn_ready[0] = head_post_fin(hp, tk2)
        for j in range(NCH1):
            tok0 = j * CH1
            if j == 0:
                P.barrier()
            do_transposes(4, 4 * j, xnT)
            if j >= 1:
                flush_posts()
            units = []
            for h in range(4):
                units.append((512 + h * 128, 1, xnT, CH1, KT[:, h, tok0:tok0 + CH1]))
            for h in range(4):
                units.append((h * 128, 0, xnT, CH1, QT[:, h, :]))
            qk_units(units)
            if j == 0:
                sv = xnT_tok[0]
                xnT_tok[0] = [xnTm_tok]
                v_proj(xnTm, NMETA, 0, 32)
                xnT_tok[0] = sv
            if j >= 1:
                P.pending["pe"] = [attn_ready[0]]
                at_transposes(tok0 - CH1)
            for s_ in range(4):
                v_proj(xnT, 128, s_ * 128, 4 * j + s_)
            P.barrier()
            if j == 1:
                for m in range(1, 8):
                    wdma(wotoks, s_wo, lambda e, m=m: e.dma_start(out=WO[:, m, :], in_=wo[m * 128:(m + 1) * 128, :]))
            if j == NCH1 - 1:
                for dc, part in W2_EARLY:
                    wdma(w2toks, s_w2, lambda e, dc=dc, part=part: e.dma_start(
                        out=W2[:, dc, part * 1536:(part + 1) * 1536],
                        in_=w_in[dc * 128:(dc + 1) * 128, 1536 + part * 1536:1536 + (part + 1) * 1536]))
            tiles = ["m"] + list(range(4 * j + 4))
            ntl = len(tiles)
            mask_eng = "pool" if j == 0 else "dve"
            stream = [(h, idx, kt) for h in range(4) for idx, kt in enumerate(tiles)]
            fbw = {h: [True] * 4 for h in range(4)}
            avq = []
            do_prep = (j + 1 < NCH1)
            prep = {}

            def emit_av(info):
                h_, nk_, qlo_, vt_, ei_, rdy, is_last = info
                jb_lo = qlo_ // 128
                flags = []
                for c in range(2):
                    for jb in range(jb_lo, 4):
                        flags.append((c, jb, fbw[h_][jb]))
                        fbw[h_][jb] = False
                def f_av(e, nk_=nk_, vt_=vt_, ei_=ei_, flags=flags, h_=h_):
                    ins = None
                    for (c, jb, st_) in flags:
                        ins = e.matmul(accv[:, jb, c, :], lhsT=et[ei_][0:nk_, c, jb * 128:(jb + 1) * 128],
                                       rhs=VA[0:nk_, vt_, h_, :], start=st_, stop=False, skip_group_check=True)
                    return ins
                tav = P.op("pe", f_av, waits=[rdy, acc_free[0]])
                et_free[ei_] = tav
                if is_last:
                    tkp = head_post_evac(tav)
                    last_evac[0] = acc_free[0]
                    posts.append((h_, tkp))
                return tav

            for g, (h, idx, kt) in enumerate(stream):
                if idx == 0 and do_prep:
                    nxt = 4 * (j + 1) + h
                    ld = x_load(nxt)
                    prep[h] = (nxt, x_sumsq(nxt, ld, h))
                if kt == "m":
                    nk, qlo, vt = NMETA, 0, 32
                    klhs = lambda c, h=h: KTm[c * 64:(c + 1) * 64, h, :]
                else:
                    nk, vt = 128, kt
                    i_loc = kt - 4 * j
                    qlo = 128 * i_loc if i_loc >= 0 else 0
                    klhs = lambda c, kt=kt, h=h: KT[c * 64:(c + 1) * 64, h, kt * 128:(kt + 1) * 128]
                b = g % 2
                def f_qk(e, b=b, nk=nk, qlo=qlo, klhs=klhs, h=h):
                    ins = None
                    for c in range(2):
                        ins = e.matmul(psum[0:nk, 2 * b + c, qlo:CH1], lhsT=klhs(c), rhs=QT[c * 64:(c + 1) * 64, h, qlo:CH1],
                                       start=True, stop=True)
                    return ins
                tqk = P.op("pe", f_qk, waits=[st_free[b]])
                ei = eti[0] % 4
                eti[0] += 1
                texp = P.op("act", lambda e, b=b, nk=nk, qlo=qlo, ei=ei: e.activation(
                    out=et[ei][0:nk, :, qlo:CH1], in_=psum[0:nk, 2 * b:2 * b + 2, qlo:CH1], func=AF.Exp, scale=QK_SCALE),
                    waits=[tqk, et_free[ei]])
                st_free[b] = texp
                last_exp[0] = texp
                ready = texp
                if kt == "m":
                    if j == 0:
                        ready = P.op(mask_eng, lambda e, ei=ei, h=h: e.tensor_tensor(
                            out=et[ei][0:NMETA, :, 0:128], in0=et[ei][0:NMETA, :, 0:128], in1=MTm[:, 2 * h:2 * h + 2, :],
                            op=ALU.mult), waits=[texp, mt_tok])
                else:
                    i_loc = kt - 4 * j
                    if i_loc >= 0:
                        w = min(256, CH1 - qlo)
                        ready = P.op(mask_eng, lambda e, ei=ei, h=h, qlo=qlo, w=w: e.tensor_tensor(
                            out=et[ei][:, :, qlo:qlo + w], in0=et[ei][:, :, qlo:qlo + w], in1=MT[:, 2 * h:2 * h + 2, 0:w],
                            op=ALU.mult), waits=[texp, mt_tok])
                    elif i_loc == -1:
                        ready = P.op(mask_eng, lambda e, ei=ei, h=h: e.tensor_tensor(
                            out=et[ei][:, :, 0:128], in0=et[ei][:, :, 0:128], in1=MT[:, 2 * h:2 * h + 2, 128:256],
                            op=ALU.mult), waits=[texp, mt_tok])
                avq.append((h, nk, qlo, vt, ei, ready, idx == ntl - 1))
                if len(avq) > 2:
                    emit_av(avq.pop(0))
                if idx == min(7, ntl - 1):
                    flush_posts()
                    if do_prep:
                        nxt, sstok = prep[h]
                        tkr = x_rstd(nxt, sstok)
                        x_norm(nxt, h, [tkr])
            while avq:
                emit_av(avq.pop(0))
            for b_ in range(4):
                bank_free[b_] = last_exp[0]
                bank_free[4 + b_] = last_evac[0]
            ring_pos[0] = 0
        flush_posts()
        if not debug:
            kv_dead = [P.last["pe"]]
            for part_set in ((0, 1), (2,)):
                for dc in range(8):
                    for part in part_set:
                        if (dc, part) in W2_EARLY:
                            continue
                        wdma(w2toks, s_w2, lambda e, dc=dc, part=part: e.dma_start(
                            out=W2[:, dc, part * 1536:(part + 1) * 1536],
                            in_=w_in[dc * 128:(dc + 1) * 128, 1536 + part * 1536:1536 + (part + 1) * 1536]), extra=kv_dead)
                if part_set == (0, 1):
                    w2_lo = w2toks[-4:]
            w2_all = w2toks[-4:]
            wbtoks = []
            s_wb = [P.dsem(f"d_wb_{i}") for i in range(4)]
            for n_ in range(2):
                for fc in range(4):
                    wdma(wbtoks, s_wb, lambda e, n_=n_, fc=fc: e.dma_start(
                        out=WB[:, n_, fc, :], in_=wb[n_ * 512 + fc * 128:n_ * 512 + (fc + 1) * 128, :]), extra=kv_dead)
            wb_all = wbtoks[-4:]
            wdma(wotoks, s_wo, lambda e: e.dma_start(out=WO[:, 0, :], in_=wo[0:128, :]), extra=kv_dead)
            wo_all = wotoks[-4:]

        P.barrier()
        at_transposes(S - CH1)

        P.barrier()
        if debug:
            s_dbg = P.dsem("d_dbg")
            dt_ = []
            def ddma(o, i):
                dt_.append(P.dma("pool", lambda e, o=o, i=i: e.dma_start(out=o, in_=i), s_dbg))
                P.wait_only("pool", [dt_[-1]])
            for h_ in range(4):
                for q_ in range(4):
                    ddma(dbg_kt[:, h_ * S + q_ * 1024:h_ * S + (q_ + 1) * 1024], KT[:, h_, q_ * 1024:(q_ + 1) * 1024])
                    ddma(dbg_at[:, h_ * S + q_ * 1024:h_ * S + (q_ + 1) * 1024], AT[:, h_, q_ * 1024:(q_ + 1) * 1024])
                ddma(dbg_qt[:, h_ * CH1:(h_ + 1) * CH1], QT[:, h_, :])
            for t_ in range(33):
                ddma(dbg_va[:, t_ * 516:(t_ + 1) * 516], VA[:, t_, :, :].rearrange("p h n -> p (h n)"))
            ddma(dbg_mt[:, :], MT[:, :, :].rearrange("p a b -> p (a b)"))
            ddma(dbg_mtm[:, :], MTm[:, :, :].rearrange("p a b -> p (a b)"))
            ddma(dbg_sm[:, 0:40], rstd_all)
            ddma(dbg_sm[:, 40:48], lam_s)
            ddma(dbg_sm[:, 48:50], gqk)
            ddma(dbg_ktm[:, :], KTm[:, :, :].rearrange("p a b -> p (a b)"))

        if debug:
            kv_dead = []
            for part_set in ((0, 1), (2,)):
                for dc in range(8):
                    for part in part_set:
                        if (dc, part) in W2_EARLY:
                            continue
                        wdma(w2toks, s_w2, lambda e, dc=dc, part=part: e.dma_start(
                            out=W2[:, dc, part * 1536:(part + 1) * 1536],
                            in_=w_in[dc * 128:(dc + 1) * 128, 1536 + part * 1536:1536 + (part + 1) * 1536]), extra=kv_dead)
                if part_set == (0, 1):
                    w2_lo = w2toks[-4:]
            w2_all = w2toks[-4:]
            wbtoks = []
            s_wb = [P.dsem(f"d_wb_{i}") for i in range(4)]
            for n_ in range(2):
                for fc in range(4):
                    wdma(wbtoks, s_wb, lambda e, n_=n_, fc=fc: e.dma_start(
                        out=WB[:, n_, fc, :], in_=wb[n_ * 512 + fc * 128:n_ * 512 + (fc + 1) * 128, :]), extra=kv_dead)
            wb_all = wbtoks[-4:]
            wdma(wotoks, s_wo, lambda e: e.dma_start(out=WO[:, 0, :], in_=wo[0:128, :]), extra=kv_dead)
            wo_all = wotoks[-4:]

        NT2 = CH2 // 128
        NCH2 = S // CH2

        def x_prep2(n, xslot):
            slot = n % 2
            ld = P.dma("sp", lambda e, n=n, slot=slot: e.dma_start(out=xr[:, slot, :], in_=x[n * 128:(n + 1) * 128, :]),
                       s_xr[slot], waits=[xr_free[slot]])
            x_norm(n, xslot, [ld, xn_free[0]])

        def proj2(woff, src, n):
            bi, bw = bank_acquire()
            def f(e, bi=bi, woff=woff, src=src, n=n):
                ins = None
                for dc in range(8):
                    ins = e.matmul(psum[:, bi, 0:n], lhsT=W2[:, dc, woff:woff + 128], rhs=src[:, dc, 0:n],
                                   start=(dc == 0), stop=(dc == 7))
                return ins
            tp = P.op("pe", f, waits=[bw] + (w2_lo if woff + 128 <= 3072 else w2_all) + xnT_tok[0])
            return bi, tp

        carry_tok = [None] * 4
        xnT2 = xnT[:, :, 0:CH2]
        for fc in range(4):
            bcc, tcc = proj2(1024 + fc * 128, xnTm, NMETA)
            bch, tch = proj2(1536 + fc * 128, xnTm, NMETA)
            tk = P.op("act", lambda e, bch=bch, fc=fc: e.activation(out=chm4[:, fc, :], in_=psum[:, bch, 0:NMETA], func=AF.Copy), waits=[tch])
            bank_free[bch] = tk
            tk = P.op("dve", lambda e, bcc=bcc, fc=fc: e.tensor_tensor(out=um4[:, fc, :], in0=psum[:, bcc, 0:NMETA], in1=chm4[:, fc, :], op=ALU.mult),
                      waits=[tk, tcc])
            bank_free[bcc] = tk
            carry_tok[fc] = P.op("dve", lambda e, fc=fc: e.tensor_copy(out=carry[:, fc, :], in_=um4[:, fc, NMETA - 2:NMETA]), after_prev=True)

        if debug:
            ddma2 = P.dma("pool", lambda e: e.dma_start(out=dbg_carry[:, :], in_=carry[:, :, :].rearrange("p a b -> p (a b)")), s_dbg,
                          waits=[carry_tok[3]])
            P.wait_only("pool", [ddma2])
            ddma2 = P.dma("pool", lambda e: e.dma_start(out=dbg_xntm[:, :], in_=xnTm[:, :, :].rearrange("p a b -> p (a b)")), s_dbg)
            P.wait_only("pool", [ddma2])
        for s_ in range(NT2):
            x_prep2(s_, s_)

        ob_store = [None, None]
        ci = [0]
        gi = [0]
        do_transposes(NT2, 0, xnT2)
        if NCH2 > 1:
            for s_ in range(NT2):
                x_prep2(NT2 + s_, s_)
        for j in range(NCH2):
            tok0 = j * CH2
            for fc in range(4):
                k = ci[0] % 2
                ci[0] += 1
                bgc, tgc = proj2(2048 + fc * 128, xnT2, CH2)
                bch, tch = proj2(1536 + fc * 128, xnT2, CH2)
                bcc, tcc = proj2(1024 + fc * 128, xnT2, CH2)
                bcb, tcb = proj2(512 + fc * 128, xnT2, CH2)
                tsg = P.op("act", lambda e, bgc=bgc, k=k: e.activation(out=sg[k], in_=psum[:, bgc, 0:CH2], func=AF.Sigmoid), waits=[tgc, rfree.get(("sg", k))])
                tchc = P.op("act", lambda e, bch=bch, k=k: e.activation(out=chb[k], in_=psum[:, bch, 0:CH2], func=AF.Copy), waits=[tch, rfree.get(("chb", k))])
                bank_free[bch] = tchc
                tcbs = P.op("act", lambda e, bcb=bcb, k=k: e.activation(out=cbs[k], in_=psum[:, bcb, 0:CH2], func=AF.Copy),
                            waits=[tcb, rfree.get(("cbs", k))])
                bank_free[bcb] = tcbs
                tsl = P.op("dve", lambda e, bgc=bgc, k=k: e.tensor_tensor(out=sg[k], in0=psum[:, bgc, 0:CH2], in1=sg[k], op=ALU.mult),
                           waits=[tsg])
                bank_free[bgc] = tsl
                P.op("dve", lambda e, fc=fc, k=k: e.tensor_copy(out=ubuf[k][:, 0:2], in_=carry[:, fc, :]), waits=[carry_tok[fc]])
                tu = P.op("dve", lambda e, bcc=bcc, k=k: e.tensor_tensor(out=ubuf[k][:, 2:CH2 + 2], in0=psum[:, bcc, 0:CH2], in1=chb[k],
                                                                       op=ALU.mult), waits=[tchc, tcc])
                bank_free[bcc] = tu
                rfree[("chb", k)] = tu
                carry_tok[fc] = P.op("dve", lambda e, fc=fc, k=k: e.tensor_copy(out=carry[:, fc, :], in_=ubuf[k][:, CH2:CH2 + 2]),
                                     after_prev=True)
                P.op("dve", lambda e, fc=fc, k=k: e.tensor_scalar(out=yb[k], in0=ubuf[k][:, 0:CH2], scalar1=cw[:, 0, fc:fc + 1],
                                                                 scalar2=None, op0=ALU.mult), waits=[rfree.get(("yb", k))], after_prev=True)
                P.op("dve", lambda e, fc=fc, k=k: e.scalar_tensor_tensor(out=yb[k], in0=ubuf[k][:, 1:CH2 + 1], scalar=cw[:, 1, fc:fc + 1],
                                                                        in1=yb[k], op0=ALU.mult, op1=ALU.add), after_prev=True)
                P.op("dve", lambda e, fc=fc, k=k: e.scalar_tensor_tensor(out=yb[k], in0=ubuf[k][:, 2:CH2 + 2], scalar=cw[:, 2, fc:fc + 1],
                                                                        in1=yb[k], op0=ALU.mult, op1=ALU.add), after_prev=True)
                ty = P.op("dve", lambda e, k=k: e.tensor_tensor(out=yb[k], in0=cbs[k], in1=yb[k], op=ALU.mult),
                          waits=[tcbs], after_prev=True)
                rfree[("cbs", k)] = ty
                tpl = P.op("pool", lambda e, fc=fc, k=k: e.tensor_tensor(out=cT[:, fc, :], in0=yb[k], in1=sg[k], op=ALU.mult), waits=[ty])
                rfree[("sg", k)] = tpl
                rfree[("yb", k)] = tpl
            for h in range(4):
                k = ci[0] % 2
                ci[0] += 1
                bga, tga = proj2(h * 128, xnT2, CH2)
                ts_ = P.op("act", lambda e, bga=bga, k=k: e.activation(out=sg2[k], in_=psum[:, bga, 0:CH2], func=AF.Sigmoid), waits=[tga, rfree.get(("sg2", k))])
                tsl = P.op("dve", lambda e, bga=bga, k=k: e.tensor_tensor(out=sg2[k], in0=psum[:, bga, 0:CH2], in1=sg2[k], op=ALU.mult),
                           waits=[ts_])
                bank_free[bga] = tsl
                tpl = P.op("pool", lambda e, h=h, k=k, tok0=tok0: e.tensor_tensor(out=aTg[:, h, :], in0=AT[:, h, tok0:tok0 + CH2], in1=sg2[k],
                                                                                 op=ALU.mult), waits=[tsl])
                rfree[("sg2", k)] = tpl
            br_ready = P.last["pool"]
            pgs = {}
            def emit_pg(m):
                kk = gi[0] % 5
                gi[0] += 1
                bg0, tg0 = proj2(2560 + m * 128, xnT2, CH2)
                bg1, tg1 = proj2(3584 + m * 128, xnT2, CH2)
                ta0 = P.op("act", lambda e, bg0=bg0, kk=kk: e.activation(out=g0b[kk], in_=psum[:, bg0, 0:CH2], func=AF.Sigmoid),
                           waits=[tg0, rfree.get(("g01", kk))])
                bank_free[bg0] = ta0
                ta1 = P.op("act", lambda e, bg1=bg1, kk=kk: e.activation(out=g1b[kk], in_=psum[:, bg1, 0:CH2], func=AF.Sigmoid), waits=[tg1])
                bank_free[bg1] = ta1
                pgs[m] = (kk, ta0, ta1)
            for m in range(4):
                emit_pg(m)
            mT_tok = [None] * 8
            for m in range(8):
                kk, ta0, ta1 = pgs.pop(m)
                by0, bw0 = bank_acquire()
                by1, bw1 = bank_acquire()
                def f_y(e, by=by0, src=aTg, n_=0, m=m):
                    ins = None
                    for fc in range(4):
                        ins = e.matmul(psum[:, by, 0:CH2], lhsT=WB[:, n_, fc, m * 128:(m + 1) * 128], rhs=src[:, fc, :],
                                       start=(fc == 0), stop=(fc == 3))
                    return ins
                ty0 = P.op("pe", f_y, waits=[bw0, br_ready] + wb_all)
                def f_y1(e, by=by1, src=cT, n_=1, m=m):
                    ins = None
                    for fc in range(4):
                        ins = e.matmul(psum[:, by, 0:CH2], lhsT=WB[:, n_, fc, m * 128:(m + 1) * 128], rhs=src[:, fc, :],
                                       start=(fc == 0), stop=(fc == 3))
                    return ins
                ty1 = P.op("pe", f_y1, waits=[bw1])
                if m + 4 < 8:
                    emit_pg(m + 4)
                if m == 3 and j + 1 < NCH2:
                    do_transposes(NT2, NT2 * (j + 1), xnT2)
                    if j + 2 < NCH2:
                        for s_ in range(NT2):
                            x_prep2(NT2 * (j + 2) + s_, s_)
                td0 = P.op("dve", lambda e, by0=by0, kk=kk: e.tensor_tensor(out=g0b[kk], in0=psum[:, by0, 0:CH2], in1=g0b[kk], op=ALU.mult),
                           waits=[ta0, ty0])
                bank_free[by0] = td0
                td1 = P.op("dve", lambda e, by1=by1, kk=kk: e.tensor_tensor(out=g1b[kk], in0=psum[:, by1, 0:CH2], in1=g1b[kk], op=ALU.mult),
                           waits=[ta1, ty1])
                bank_free[by1] = td1
                tpl = P.op("pool", lambda e, m=m, kk=kk: e.tensor_tensor(out=mT[:, m, :], in0=g0b[kk], in1=g1b[kk], op=ALU.add), waits=[td1])
                rfree[("g01", kk)] = tpl
                mT_tok[m] = tpl
            groups = []
            lds = {}
            for s_ in range(NT2):
                n = NT2 * j + s_
                slot = n % 2
                lds[s_] = P.dma("sp", lambda e, n=n, slot=slot: e.dma_start(out=ob[slot], in_=x[n * 128:(n + 1) * 128, :]),
                                s_obl[slot], waits=[ob_store[slot]])
                for nh in range(2):
                    bi, bw = bank_acquire()
                    groups.append((s_, nh, bi, bw))
            tp = None
            for m in range(8):
                def f_o(e, m=m, groups=tuple(groups)):
                    ins = None
                    for (s_, nh, bi, bw) in groups:
                        ins = e.matmul(psum[:, bi, :], lhsT=mT[:, m, s_ * 128:(s_ + 1) * 128], rhs=WO[:, m, nh * 512:(nh + 1) * 512],
                                       start=(m == 0), stop=(m == 7))
                    return ins
                w_ = [mT_tok[m]]
                if m == 0:
                    w_ += [g[3] for g in groups] + wo_all
                tp = P.op("pe", f_o, waits=w_)
            for s_ in range(NT2):
                n = NT2 * j + s_
                slot = n % 2
                tl = None
                for (s2, nh, bi, bw) in groups:
                    if s2 != s_:
                        continue
                    tl = P.op("dve", lambda e, bi=bi, slot=slot, nh=nh: e.tensor_tensor(
                        out=ob[slot][:, nh * 512:(nh + 1) * 512], in0=psum[:, bi, :], in1=ob[slot][:, nh * 512:(nh + 1) * 512],
                        op=ALU.add), waits=[tp, lds[s_]])
                    bank_free[bi] = tl
                ob_store[slot] = P.dma("sp", lambda e, n=n, slot=slot: e.dma_start(out=out[n * 128:(n + 1) * 128, :], in_=ob[slot]),
                                       s_obs[slot], waits=[tl])
        P.wait_only("sp", [ob_store[0], ob_store[1]])

        with nc.Block() as block:
            @block.tensor
            def _(e):
                P.replay("pe", e)

            @block.scalar
            def _(e):
                P.replay("act", e)

            @block.vector
            def _(e):
                P.replay("dve", e)

            @block.gpsimd
            def _(e):
                P.replay("pool", e)

            @block.sync
            def _(e):
                P.replay("sp", e)
    return nc


_CACHE = {}


def _consts():
    d = np.arange(NG)
    b = _rel_bucket(d)
    ohm = np.zeros((32, NG), np.float32)
    ohm[b, d] = 1.0
    ident = np.eye(128, dtype=np.float32)
    bones = np.zeros((128, 128), np.float32)
    bones[:64, :64] = 1.0
    bones[64:, 64:] = 1.0
    return ohm, ident, bones


def run_debug(inputs, ncores=1):
    ins = dict(inputs)
    f = lambda a: np.ascontiguousarray(np.asarray(a, dtype=np.float32))
    x = f(ins["x"])
    ohm, ident, bones = _consts()
    shared = {
        "meta": f(ins["meta_tokens"]), "relb": f(ins["rel_bias"]).reshape(32, 8), "norm_g": f(ins["norm_g"]).reshape(1, D),
        "w_in": f(ins["w_in"]).reshape(D, 6144), "qg": f(ins["q_norm_g"]).reshape(1, 64), "kg": f(ins["k_norm_g"]).reshape(1, 64),
        "lq1": f(ins["lambda_q1"]).reshape(1, 64), "lk1": f(ins["lambda_k1"]).reshape(1, 64),
        "lq2": f(ins["lambda_q2"]).reshape(1, 64), "lk2": f(ins["lambda_k2"]).reshape(1, 64),
        "subg": f(ins["subln_g"]).reshape(1, 128), "convw": f(ins["conv_w"]).reshape(3, 512),
        "wb": f(ins["w_branch"]).reshape(1024, 1024), "wo": f(ins["w_out"]).reshape(1024, 1024),
        "oh": ohm, "ident": ident, "bones": bones,
    }
    nc = build_program(debug=True)
    in_maps = [dict(shared, x=x[b]) for b in range(ncores)]
    res = run_bass_kernel_spmd(nc, in_maps, core_ids=list(range(ncores)))
    return res.results


def kernel(x, meta_tokens, rel_bias, norm_g, w_in, q_norm_g, k_norm_g, lambda_q1, lambda_k1,
           lambda_q2, lambda_k2, subln_g, conv_w, w_branch, w_out):
    f = lambda a: np.ascontiguousarray(np.asarray(a, dtype=np.float32))
    x = f(x)
    ohm, ident, bones = _consts()
    shared = {
        "meta": f(meta_tokens), "relb": f(rel_bias).reshape(32, 8), "norm_g": f(norm_g).reshape(1, D),
        "w_in": f(w_in).reshape(D, 6144), "qg": f(q_norm_g).reshape(1, 64), "kg": f(k_norm_g).reshape(1, 64),
        "lq1": f(lambda_q1).reshape(1, 64), "lk1": f(lambda_k1).reshape(1, 64),
        "lq2": f(lambda_q2).reshape(1, 64), "lk2": f(lambda_k2).reshape(1, 64),
        "subg": f(subln_g).reshape(1, 128), "convw": f(conv_w).reshape(3, 512),
        "wb": f(w_branch).reshape(1024, 1024), "wo": f(w_out).reshape(1024, 1024),
        "oh": ohm, "ident": ident, "bones": bones,
    }
    if "nc" not in _CACHE:
        _CACHE["nc"] = build_program()
    nc = _CACHE["nc"]
    in_maps = [dict(shared, x=x[b]) for b in range(NCORES)]
    res = run_bass_kernel_spmd(nc, in_maps, core_ids=list(range(NCORES)))
    return np.stack([np.asarray(r["out"], dtype=np.float32) for r in res.results], axis=0)
```

```python
import math
from contextlib import ExitStack

import numpy as np
import concourse.bass as bass
import concourse.mybir as mybir
from concourse.bass_utils import run_bass_kernel_spmd

F32 = mybir.dt.float32
BF16 = mybir.dt.bfloat16
AF = mybir.ActivationFunctionType
ALU = mybir.AluOpType
AX = mybir.AxisListType

S = 4096
D = 1024
NMETA = 16
NCORES = 8
CH1 = 512
CH2 = 256
NG = 272
GV = 400
EPS = 1e-6
LAM_INIT = 0.8 - 0.6 * math.exp(-0.3 * 0)
QK_SCALE = 64 ** -0.5
ARENA_BYTES = 212800


def _rel_bucket(n):
    n = np.asarray(n, dtype=np.int32)
    max_exact = 16
    nf = np.maximum(n, max_exact).astype(np.float32)
    large = max_exact + (np.log(nf / np.float32(max_exact)) / np.float32(math.log(128 / 16))
                         * np.float32(32 - max_exact)).astype(np.int32)
    large = np.minimum(large, 31)
    return np.where(n < max_exact, n, large)


class DSem:
    def __init__(self, handle):
        self.h = handle
        self.count = 0


class Plan:
    ENGS = ("pe", "act", "dve", "pool", "sp")

    def __init__(self, nc, es):
        self.nc = nc
        self.es = es
        self.ops = {e: [] for e in self.ENGS}
        self.sem = {e: es.enter_context(nc.semaphore("s_" + e)) for e in ("pe", "act", "dve", "pool")}
        self.cnt = {e: 0 for e in self.sem}
        self.last = {e: None for e in self.sem}
        self.pending = {e: [] for e in self.ENGS}

    def dsem(self, name):
        return DSem(self.es.enter_context(self.nc.semaphore(name)))

    def op(self, eng, fn, waits=(), after_prev=False):
        w = [t for t in waits if t is not None] + self.pending[eng]
        if after_prev and self.last[eng] is not None:
            w.append(self.last[eng])
        self.pending[eng] = []
        self.cnt[eng] += 1
        tok = (eng, self.sem[eng], self.cnt[eng])
        self.ops[eng].append((fn, w, tok, 1))
        self.last[eng] = tok
        return tok

    def dma(self, eng, fn, dsem, waits=()):
        w = [t for t in waits if t is not None] + self.pending[eng]
        self.pending[eng] = []
        dsem.count += 16
        tok = ("dma", dsem.h, dsem.count)
        self.ops[eng].append((fn, w, tok, 16))
        return tok

    def wait_only(self, eng, waits):
        self.ops[eng].append((None, [t for t in waits if t is not None], None, 0))

    def barrier(self):
        toks = [t for t in self.last.values() if t is not None]
        for e in self.ENGS:
            self.pending[e] = list(toks)

    def replay(self, eng, e):
        waited = {}
        for fn, waits, tok, inc in self.ops[eng]:
            for (src, s, v) in waits:
                if src == eng and eng == "pe":
                    continue
                k = id(s)
                if waited.get(k, 0) < v:
                    e.wait_ge(s, v)
                    waited[k] = v
            if fn is None:
                continue
            ins = fn(e)
            if tok is not None:
                ins.then_inc(tok[1], inc)


class Arena:
    def __init__(self, ar, nbytes):
        self.ar = ar
        self.n = nbytes
        self.top = 0

    def alloc(self, shape, dtype):
        esz = 2 if dtype == BF16 else 4
        per = int(np.prod(shape[1:])) * esz
        off = (self.top + 31) // 32 * 32
        per4 = (per + 3) // 4 * 4
        assert off + per4 <= self.n, f"arena overflow: {off + per4} > {self.n}"
        self.top = off + per4
        ap = self.ar[:, off // 4:(off + per4) // 4]
        if dtype == BF16:
            ap = ap.bitcast(BF16)
            ap = ap[:, 0:per // 2]
        if len(shape) == 3:
            ap = ap.rearrange("p (a b) -> p a b", a=shape[1])
        elif len(shape) == 4:
            ap = ap.rearrange("p (a b c) -> p a b c", a=shape[1], b=shape[2])
        if shape[0] < 128:
            ap = ap[0:shape[0]]
        return ap


def bcast_rows(dram_ap, nparts, ncols, off=0):
    return bass.AP(dram_ap.tensor, off, [[0, nparts], [1, ncols]])


def build_program(debug=False):
    nc = bass.Bass("TRN2", target_bir_lowering=False)
    dt_in = lambda name, shape: nc.dram_tensor(name, shape, F32, kind="ExternalInput").ap()
    x = dt_in("x", [S, D])
    meta = dt_in("meta", [NMETA, D])
    relb = dt_in("relb", [32, 8])
    norm_g = dt_in("norm_g", [1, D])
    w_in = dt_in("w_in", [D, 6144])
    qg = dt_in("qg", [1, 64])
    kg = dt_in("kg", [1, 64])
    lq1 = dt_in("lq1", [1, 64])
    lk1 = dt_in("lk1", [1, 64])
    lq2 = dt_in("lq2", [1, 64])
    lk2 = dt_in("lk2", [1, 64])
    subg = dt_in("subg", [1, 128])
    convw = dt_in("convw", [3, 512])
    wb = dt_in("wb", [1024, 1024])
    wo = dt_in("wo", [1024, 1024])
    oh = dt_in("oh", [32, NG])
    ident_d = dt_in("ident", [128, 128])
    bones_d = dt_in("bones", [128, 128])
    out = nc.dram_tensor("out", [S, D], F32, kind="ExternalOutput").ap()
    gscr = nc.dram_tensor("gscr", [8, GV], F32).ap()
    if debug:
        dbg_kt = nc.dram_tensor("dbg_kt", [128, 4 * S], F32, kind="ExternalOutput").ap()
        dbg_at = nc.dram_tensor("dbg_at", [128, 4 * S], F32, kind="ExternalOutput").ap()
        dbg_va = nc.dram_tensor("dbg_va", [128, 33 * 516], F32, kind="ExternalOutput").ap()
        dbg_mt = nc.dram_tensor("dbg_mt", [128, 2048], F32, kind="ExternalOutput").ap()
        dbg_mtm = nc.dram_tensor("dbg_mtm", [NMETA, 1024], F32, kind="ExternalOutput").ap()
        dbg_qt = nc.dram_tensor("dbg_qt", [128, 4 * CH1], F32, kind="ExternalOutput").ap()
        dbg_sm = nc.dram_tensor("dbg_sm", [128, 96], F32, kind="ExternalOutput").ap()
        dbg_ktm = nc.dram_tensor("dbg_ktm", [128, 64], F32, kind="ExternalOutput").ap()
        dbg_carry = nc.dram_tensor("dbg_carry", [128, 8], F32, kind="ExternalOutput").ap()
        dbg_xntm = nc.dram_tensor("dbg_xntm", [128, 128], F32, kind="ExternalOutput").ap()
    gscr2 = nc.dram_tensor("gscr2", [8, 128, GV], F32).ap()

    es = ExitStack()
    with es:
        arena_t = es.enter_context(nc.sbuf_tensor("arena", [128, ARENA_BYTES // 4], F32))
        psum = es.enter_context(nc.psum_tensor("psum", [128, 8, 512], F32))
        P = Plan(nc, es)
        A = Arena(arena_t, ARENA_BYTES)

        AT = A.alloc([128, 4, S], BF16)
        big = A.alloc([128, 53248], BF16)
        xr = A.alloc([128, 2, D], F32)
        xn = A.alloc([128, 4, D], BF16)
        xnT = A.alloc([128, 8, CH1], BF16)
        gb = A.alloc([128, D], F32)
        gqk = A.alloc([128, 2], F32)
        gsub = A.alloc([128, 128], F32)
        lamv = A.alloc([128, 4, 64], F32)
        lam_s = A.alloc([128, 8], F32)
        cw = A.alloc([128, 3, 4], F32)
        ident = A.alloc([128, 128], BF16)
        bones = A.alloc([128, 128], BF16)
        xnTm = A.alloc([128, 8, NMETA], BF16)
        rstd_all = A.alloc([128, 40], F32)
        ssq = A.alloc([128, 40], F32)
        frame = A.top

        W1 = big[:, 0:12288].rearrange("p (a b) -> p a b", a=8)
        KT = big[:, 12288:28672].rearrange("p (a b) -> p a b", a=4)
        VA = big[:, 28672:45700].rearrange("p (t h n) -> p t h n", t=33, h=4)
        KTm = big[:, 45700:45764].rearrange("p (a b) -> p a b", a=4)
        W2 = big[:, 0:36864].rearrange("p (a b) -> p a b", a=8)
        WB = big[:, 36864:45056].rearrange("p (n f c) -> p n f c", n=2, f=4)
        WO = big[:, 45056:53248].rearrange("p (a b) -> p a b", a=8)

        QT = A.alloc([128, 4, CH1], BF16)
        sq = [A.alloc([128, CH1], BF16) for _ in range(2)]
        lnr = [A.alloc([128, CH1], F32) for _ in range(2)]
        et_off = A.top
        et = [A.alloc([128, 2, CH1], BF16) for _ in range(4)]
        attn = A.alloc([128, 4, 512], BF16)
        setup_end = A.top
        MT = A.alloc([128, 8, 256], F32)
        MTm = A.alloc([NMETA, 8, 128], F32)
        att = A.alloc([128, 4, 128], F32)
        accs = A.alloc([128, 4, 258], F32)
        rz = A.alloc([128, 4, 2], F32)
        ssum = A.alloc([128, 4], F32)
        rs = A.alloc([128, 4], F32)
        junk = A.alloc([128, 64], F32)
        s1_top = A.top
        A.top = et_off
        xm = A.alloc([NMETA, D], F32)
        xnm = A.alloc([NMETA, D], BF16)
        gv = A.alloc([8, GV], F32)
        gd = A.alloc([8, NG], F32)
        rb = A.alloc([32, 8], F32)
        ohs = A.alloc([32, NG], F32)
        assert A.top <= setup_end, (A.top, setup_end)
        A.top = s1_top

        A.top = frame
        sg = [A.alloc([128, CH2], F32) for _ in range(2)]
        chb = [A.alloc([128, CH2], F32) for _ in range(2)]
        ubuf = [A.alloc([128, CH2 + 2], F32) for _ in range(2)]
        yb = [A.alloc([128, CH2], F32) for _ in range(2)]
        sg2 = [A.alloc([128, CH2], F32) for _ in range(2)]
        g0b = [A.alloc([128, CH2], F32) for _ in range(5)]
        g1b = [A.alloc([128, CH2], F32) for _ in range(5)]
        cbs = [A.alloc([128, CH2], F32) for _ in range(2)]
        cT = A.alloc([128, 4, CH2], BF16)
        aTg = A.alloc([128, 4, CH2], BF16)
        mT = A.alloc([128, 8, CH2], BF16)
        ob = [A.alloc([128, D], F32) for _ in range(2)]
        carry = A.alloc([128, 4, 2], F32)
        chm4 = A.alloc([128, 4, NMETA], F32)
        um4 = A.alloc([128, 4, NMETA], F32)

        pbf = psum[:, :, :].rearrange("p a b -> p (a b)").bitcast(BF16).rearrange("p (a b) -> p a b", a=8)

        s_const = P.dsem("d_const")
        s_w1 = [P.dsem(f"d_w1_{i}") for i in range(4)]
        s_w2 = [P.dsem(f"d_w2_{i}") for i in range(4)]
        s_gs = P.dsem("d_gs")
        s_mt = P.dsem("d_mt")
        s_xr = [P.dsem(f"d_xr{i}") for i in range(2)]
        s_obl = [P.dsem(f"d_obl{i}") for i in range(2)]
        s_obs = [P.dsem(f"d_obs{i}") for i in range(2)]

        bank_free = [None] * 8
        ring_pos = [0]

        def bank_acquire():
            i = ring_pos[0] % 8
            ring_pos[0] += 1
            return i, bank_free[i]

        ctoks = []
        def cdma(out_ap, in_ap, slow=False):
            kw = {"allow_slow_non_contiguous": True} if slow else {}
            ctoks.append(P.dma("sp", lambda e, o=out_ap, i=in_ap, kw=kw: e.dma_start(out=o, in_=i, **kw), s_const))
        cdma(gb, bcast_rows(norm_g, 128, D))
        cdma(xm, meta[:, :])
        cdma(gqk[0:64, 0:1], bass.AP(qg.tensor, 0, [[1, 64], [1, 1]]))
        cdma(gqk[64:128, 0:1], bass.AP(qg.tensor, 0, [[1, 64], [1, 1]]))
        cdma(gqk[0:64, 1:2], bass.AP(kg.tensor, 0, [[1, 64], [1, 1]]))
        cdma(gqk[64:128, 1:2], bass.AP(kg.tensor, 0, [[1, 64], [1, 1]]))
        cdma(gsub, bcast_rows(subg, 128, 128))
        for i, v in enumerate((lq1, lk1, lq2, lk2)):
            cdma(lamv[:, i, :], bcast_rows(v, 128, 64))
        for j_ in range(3):
            for fc_ in range(4):
                cdma(cw[:, j_, fc_:fc_ + 1], bass.AP(convw.tensor, j_ * 512 + fc_ * 128, [[1, 128], [1, 1]]))
        cdma(rb, relb[:, :])
        cdma(ohs, oh[:, :])
        const_tok = ctoks[-1]

        w1toks = []
        def wdma(toks, sems, fn, extra=()):
            i = len(toks)
            toks.append(P.dma("pool", fn, sems[i % 4], waits=([toks[i - 4]] if i >= 4 else []) + list(extra)))
        wdma(w1toks, s_w1, lambda e: e.dma_start(out=ident, in_=ident_d[:, :]))
        wdma(w1toks, s_w1, lambda e: e.dma_start(out=bones, in_=bones_d[:, :]))
        w1_grp = {}
        for gname, c0 in (("k", 512), ("q", 0), ("v", 1024)):
            for dc in range(8):
                wdma(w1toks, s_w1, lambda e, dc=dc, c0=c0: e.dma_start(out=W1[:, dc, c0:c0 + 512],
                                                                     in_=w_in[dc * 128:(dc + 1) * 128, c0:c0 + 512]))
            w1_grp[gname] = w1toks[-4:]
        w1_id = w1toks[0:2]
        wotoks = []
        s_wo = [P.dsem(f"d_wo_{i}") for i in range(4)]

        P.op("pool", lambda e: e.memset(VA[:, :, :, 128:129], 1.0))
        P.op("pool", lambda e: e.memset(gv, 0.0))

        t = P.op("dve", lambda e: e.scalar_tensor_tensor(out=junk, in0=lamv[:, 0, :], scalar=1.0, in1=lamv[:, 1, :],
                                                         op0=ALU.mult, op1=ALU.mult, accum_out=lam_s[:, 0:1]),
                 waits=[const_tok])
        t = P.op("dve", lambda e: e.scalar_tensor_tensor(out=junk, in0=lamv[:, 2, :], scalar=1.0, in1=lamv[:, 3, :],
                                                         op0=ALU.mult, op1=ALU.mult, accum_out=lam_s[:, 1:2]))
        t = P.op("act", lambda e: e.activation(out=lam_s[:, 2:4], in_=lam_s[:, 0:2], func=AF.Exp), waits=[t])
        t = P.op("dve", lambda e: e.tensor_tensor(out=lam_s[:, 4:5], in0=lam_s[:, 2:3], in1=lam_s[:, 3:4], op=ALU.subtract), waits=[t])
        t = P.op("dve", lambda e: e.tensor_scalar(out=lam_s[:, 5:6], in0=lam_s[:, 4:5], scalar1=LAM_INIT, scalar2=-1.0,
                                                  op0=ALU.add, op1=ALU.mult), after_prev=True)
        neglam = lam_s[:, 5:6]
        P.op("dve", lambda e: e.tensor_scalar(out=gsub, in0=gsub, scalar1=1.0 - LAM_INIT, scalar2=None, op0=ALU.mult))


        xr_free = [None, None]
        xn_free = [None]
        xn_ready = {}

        def x_load(n):
            slot = n % 2
            tk = P.dma("sp", lambda e, n=n, slot=slot: e.dma_start(out=xr[:, slot, :], in_=x[n * 128:(n + 1) * 128, :]),
                       s_xr[slot], waits=[xr_free[slot]])
            return tk

        def x_sumsq(n, ldtok, xslot):
            slot = n % 2
            return P.op("dve", lambda e, n=n, slot=slot, xslot=xslot: e.scalar_tensor_tensor(
                out=xn[:, xslot, :], in0=xr[:, slot, :], scalar=1.0, in1=xr[:, slot, :],
                op0=ALU.mult, op1=ALU.mult, accum_out=ssq[:, n:n + 1]), waits=[ldtok, xn_free[0]])

        def x_rstd(n, tk):
            t1 = P.op("act", lambda e, n=n: e.activation(out=rstd_all[:, n:n + 1], in_=ssq[:, n:n + 1], func=AF.Ln,
                                                         scale=1.0 / D, bias=EPS), waits=[tk])
            return P.op("act", lambda e, n=n: e.activation(out=rstd_all[:, n:n + 1], in_=rstd_all[:, n:n + 1], func=AF.Exp,
                                                           scale=-0.5), after_prev=True)

        def x_norm(n, xslot, waits):
            slot = n % 2
            tk = P.op("dve", lambda e, n=n, slot=slot, xslot=xslot: e.scalar_tensor_tensor(
                out=xn[:, xslot, :], in0=xr[:, slot, :], scalar=rstd_all[:, n:n + 1], in1=gb,
                op0=ALU.mult, op1=ALU.mult), waits=waits)
            xr_free[slot] = tk
            xn_ready[n] = tk
            return tk

        t = P.op("dve", lambda e: e.scalar_tensor_tensor(out=xnm, in0=xm, scalar=1.0, in1=xm, op0=ALU.mult,
                                                         op1=ALU.mult, accum_out=ssq[0:NMETA, 32:33]), waits=[const_tok])
        t = P.op("act", lambda e: e.activation(out=rstd_all[0:NMETA, 32:33], in_=ssq[0:NMETA, 32:33], func=AF.Ln,
                                               scale=1.0 / D, bias=EPS), waits=[t])
        t = P.op("act", lambda e: e.activation(out=rstd_all[0:NMETA, 32:33], in_=rstd_all[0:NMETA, 32:33], func=AF.Exp, scale=-0.5), after_prev=True)
        t = P.op("dve", lambda e: e.scalar_tensor_tensor(out=xnm, in0=xm, scalar=rstd_all[0:NMETA, 32:33], in1=gb[0:NMETA, :],
                                                         op0=ALU.mult, op1=ALU.mult), waits=[t])
        bi, bw = bank_acquire()
        def f_mt(e, bi=bi):
            ins = None
            for dc in range(8):
                ins = e.transpose(out=pbf[:, bi, dc * NMETA:(dc + 1) * NMETA], in_=xnm[:, dc * 128:(dc + 1) * 128],
                                  identity=ident[0:NMETA, 0:NMETA])
            return ins
        t = P.op("pe", f_mt, waits=[t, bw] + w1_id)
        t = P.op("dve", lambda e, bi=bi: e.tensor_copy(out=xnTm, in_=pbf[:, bi, 0:8 * NMETA].rearrange("p (a b) -> p a b", a=8)),
                 waits=[t])
        bank_free[bi] = t
        xnTm_tok = t

        def do_transposes(ntiles, tile0, xnT_dst):
            last = None
            evs = []
            for s_ in range(ntiles):
                bi, bw = bank_acquire()
                def f(e, bi=bi, s_=s_):
                    ins = None
                    for dc in range(8):
                        ins = e.transpose(out=pbf[:, bi, dc * 128:(dc + 1) * 128], in_=xn[:, s_, dc * 128:(dc + 1) * 128],
                                          identity=ident)
                    return ins
                tp = P.op("pe", f, waits=[xn_ready[tile0 + s_], bw])
                src = pbf[:, bi, :].rearrange("p (a b) -> p a b", a=8)
                dst = xnT_dst[:, :, s_ * 128:(s_ + 1) * 128]
                tc_ = P.op("act", lambda e, src=src, dst=dst: e.activation(out=dst, in_=src, func=AF.Copy), waits=[tp])
                bank_free[bi] = tc_
                evs.append(tc_)
                last = tp
            xn_free[0] = last
            xnT_tok[0] = evs
            return last

        sqi = [0]
        xnT_tok = [[xnTm_tok]]
        rfree = {}

        def qk_units(units):
            st = []
            for u in range(len(units) + 1):
                if u < len(units):
                    woff, gcol, src, n, dest = units[u]
                    bi, bw = bank_acquire()
                    def f(e, bi=bi, woff=woff, src=src, n=n):
                        ins = None
                        for dc in range(8):
                            ins = e.matmul(psum[:, bi, 0:n], lhsT=W1[:, dc, woff:woff + 128], rhs=src[:, dc, 0:n],
                                           start=(dc == 0), stop=(dc == 7))
                        return ins
                    tp = P.op("pe", f, waits=[bw] + w1_grp["k" if woff >= 512 else "q"] + xnT_tok[0])
                    k = sqi[0] % 2
                    sqi[0] += 1
                    tsq = P.op("act", lambda e, bi=bi, n=n, k=k: e.activation(out=sq[k][:, 0:n], in_=psum[:, bi, 0:n], func=AF.Square),
                               waits=[tp])
                    st.append((bi, k, tsq, gcol, n, dest))
                if u >= 1:
                    bi, k, tsq, gcol, n, dest = st[u - 1]
                    b2, bw2 = bank_acquire()
                    tss = P.op("pe", lambda e, b2=b2, k=k, n=n: e.matmul(psum[:, b2, 0:n], lhsT=bones, rhs=sq[k][:, 0:n],
                                                                          start=True, stop=True), waits=[tsq, bw2])
                    P.op("act", lambda e, b2=b2, k=k, n=n: e.activation(out=lnr[k][:, 0:n], in_=psum[:, b2, 0:n], func=AF.Ln,
                                                                       scale=1.0 / 64, bias=EPS), waits=[tss, rfree.get(("lnr", k))])
                    tr = P.op("act", lambda e, k=k, n=n: e.activation(out=lnr[k][:, 0:n], in_=lnr[k][:, 0:n], func=AF.Exp, scale=-0.5), after_prev=True)
                    bank_free[b2] = tr
                    td = P.op("dve", lambda e, bi=bi, k=k, n=n, gcol=gcol, dest=dest: e.scalar_tensor_tensor(
                        out=dest, in0=psum[:, bi, 0:n], scalar=gqk[:, gcol:gcol + 1], in1=lnr[k][:, 0:n],
                        op0=ALU.mult, op1=ALU.mult), waits=[tr])
                    bank_free[bi] = td
                    rfree[("lnr", k)] = td

        def v_proj(src, ntok, s_off, vtile):
            bi, bw = bank_acquire()
            def f(e, bi=bi):
                ins = None
                for dc in range(8):
                    ins = e.matmul(psum[0:ntok, bi, :], lhsT=src[:, dc, s_off:s_off + ntok], rhs=W1[:, dc, 1024:1536],
                                   start=(dc == 0), stop=(dc == 7))
                return ins
            tp = P.op("pe", f, waits=[bw] + w1_grp["v"] + xnT_tok[0])
            tv = P.op("dve", lambda e, bi=bi: e.tensor_copy(out=VA[0:ntok, vtile, :, 0:128],
                                                           in_=psum[0:ntok, bi, :].rearrange("p (h n) -> p h n", h=4)), waits=[tp])
            bank_free[bi] = tv

        def at_transposes(tok0):
            for jb in range(4):
                bi, bw = bank_acquire()
                def f(e, bi=bi, jb=jb):
                    ins = None
                    for h in range(4):
                        ins = e.transpose(out=pbf[:, bi, h * 128:(h + 1) * 128], in_=attn[:, jb, h * 128:(h + 1) * 128], identity=ident)
                    return ins
                tp = P.op("pe", f, waits=[bw])
                src = pbf[:, bi, 0:512].rearrange("p (a b) -> p a b", a=4)
                dst = AT[:, :, tok0 + jb * 128: tok0 + (jb + 1) * 128]
                tc_ = P.op("act", lambda e, src=src, dst=dst: e.activation(out=dst, in_=src, func=AF.Copy), waits=[tp])
                bank_free[bi] = tc_

        qk_units([(512 + h * 128, 1, xnTm, NMETA, KTm[:, h, :]) for h in range(4)])

        for n in range(4):
            ld = x_load(n)
            tk = x_sumsq(n, ld, n)
            tk = x_rstd(n, tk)
            x_norm(n, n, [tk])

        bi, bw = bank_acquire()
        t = P.op("pe", lambda e, bi=bi: e.matmul(psum[0:8, bi, 0:NG], lhsT=rb[:, :], rhs=ohs[:, :], start=True, stop=True),
                 waits=[const_tok, bw])
        t = P.op("dve", lambda e, bi=bi: e.tensor_scalar(out=gd, in0=psum[0:8, bi, 0:NG], scalar1=psum[0:8, bi, NG - 1:NG],
                                                  scalar2=None, op0=ALU.subtract), waits=[t])
        bank_free[bi] = t
        t = P.op("act", lambda e: e.activation(out=gv[:, 127:127 + NG], in_=gd, func=AF.Exp), waits=[t, P.last["pool"]])
        t = P.dma("sp", lambda e: e.dma_start(out=gscr[:, :], in_=gv), s_gs, waits=[t])
        t = P.dma("sp", lambda e: e.dma_start(out=gscr2[:, :, :], in_=bass.AP(gscr.tensor, 0, [[GV, 8], [0, 128], [1, GV]])),
                  s_gs, waits=[t])
        mtoks = []
        for hc in range(8):
            mtoks.append(P.dma("sp", lambda e, hc=hc: e.dma_start(
                out=MT[:, hc, :], in_=bass.AP(gscr2.tensor, hc * 128 * GV + 127, [[GV - 1, 128], [1, 256]])), s_mt, waits=[t]))
            mtoks.append(P.dma("sp", lambda e, hc=hc: e.dma_start(
                out=MTm[:, hc, :], in_=bass.AP(gscr2.tensor, hc * 128 * GV + 127 + NMETA, [[GV - 1, NMETA], [1, 128]])), s_mt, waits=[t]))
        mt_tok = mtoks[-1]

        acc = psum[:, 4:8, :]
        accv = acc[:, :, 0:258].rearrange("p j (c n) -> p j c n", c=2)
        st_free = [None, None]
        et_free = [None] * 4
        eti = [0]
        acc_free = [None]
        NCH1 = S // CH1
        W2_EARLY = [(0, 0), (0, 1), (0, 2), (1, 0), (1, 1), (1, 2), (2, 0), (2, 1)]
        w2toks = []

        accsv = accs.rearrange("p j (c n) -> p j c n", c=2)

        def head_post_evac(last_av):
            t0 = P.op("dve", lambda e: e.tensor_copy(out=accs, in_=acc[:, :, 0:258]), waits=[last_av])
            acc_free[0] = t0
            P.op("dve", lambda e: e.reciprocal(out=rz, in_=accsv[:, :, :, 128]), after_prev=True)
            P.op("dve", lambda e: e.tensor_scalar(out=rz[:, :, 1], in0=rz[:, :, 1], scalar1=neglam, scalar2=None, op0=ALU.mult),
                 after_prev=True)
            for jb in range(4):
                P.op("dve", lambda e, jb=jb: e.tensor_scalar(out=accsv[:, jb, 1, 0:128], in0=accsv[:, jb, 1, 0:128], scalar1=rz[:, jb, 1:2],
                                                            scalar2=None, op0=ALU.mult), after_prev=(jb == 0))
            for jb in range(4):
                P.op("dve", lambda e, jb=jb: e.scalar_tensor_tensor(out=att[:, jb, :], in0=accsv[:, jb, 0, 0:128],
                                                                    scalar=rz[:, jb, 0:1], in1=accsv[:, jb, 1, 0:128],
                                                                    op0=ALU.mult, op1=ALU.add), after_prev=(jb == 0))
            P.op("dve", lambda e: e.tensor_tensor(out=accsv[:, :, 1, 0:128], in0=att, in1=att, op=ALU.mult), after_prev=True)
            return P.op("dve", lambda e: e.tensor_reduce(out=ssum, in_=accsv[:, :, 1, 0:128], axis=AX.X, op=ALU.add), after_prev=True)

        def head_post_act(tk):
            P.op("act", lambda e: e.activation(out=rs, in_=ssum, func=AF.Ln, scale=1.0 / 128, bias=EPS), waits=[tk])
            return P.op("act", lambda e: e.activation(out=rs, in_=rs, func=AF.Exp, scale=-0.5), after_prev=True)

        def head_post_fin(h, tk):
            tl = None
            for jb in range(4):
                tl = P.op("dve", lambda e, jb=jb, h=h: e.scalar_tensor_tensor(
                    out=attn[:, jb, h * 128:(h + 1) * 128], in0=att[:, jb, :], scalar=rs[:, jb:jb + 1], in1=gsub,
                    op0=ALU.mult, op1=ALU.mult), waits=[tk] if jb == 0 else [])
            return tl

        attn_ready = [None]
        last_exp = [None]
        last_evac = [None]
        posts = []

        def flush_posts():
            while posts:
                hp, tkp = posts.pop(0)
                tk2 = head_post_act(tkp)
                attn_ready[0] = head_post_fin(hp, tk2)
        for j in range(NCH1):
            tok0 = j * CH1
            if j == 0:
                P.barrier()
            do_transposes(4, 4 * j, xnT)
            if j >= 1:
                flush_posts()
            units = []
            for h in range(4):
                units.append((512 + h * 128, 1, xnT, CH1, KT[:, h, tok0:tok0 + CH1]))
            for h in range(4):
                units.append((h * 128, 0, xnT, CH1, QT[:, h, :]))
            qk_units(units)
            if j == 0:
                sv = xnT_tok[0]
                xnT_tok[0] = [xnTm_tok]
                v_proj(xnTm, NMETA, 0, 32)
                xnT_tok[0] = sv
            if j >= 1:
                P.pending["pe"] = [attn_ready[0]]
                at_transposes(tok0 - CH1)
            for s_ in range(4):
                v_proj(xnT, 128, s_ * 128, 4 * j + s_)
            P.barrier()
            if j == 1:
                for m in range(1, 8):
                    wdma(wotoks, s_wo, lambda e, m=m: e.dma_start(out=WO[:, m, :], in_=wo[m * 128:(m + 1) * 128, :]))
            if j == NCH1 - 1:
                for dc, part in W2_EARLY:
                    wdma(w2toks, s_w2, lambda e, dc=dc, part=part: e.dma_start(
                        out=W2[:, dc, part * 1536:(part + 1) * 1536],
                        in_=w_in[dc * 128:(dc + 1) * 128, 1536 + part * 1536:1536 + (part + 1) * 1536]))
            tiles = ["m"] + list(range(4 * j + 4))
            ntl = len(tiles)
            mask_eng = "pool" if j == 0 else "dve"
            stream = [(h, idx, kt) for h in range(4) for idx, kt in enumerate(tiles)]
            fbw = {h: [True] * 4 for h in range(4)}
            avq = []
            do_prep = (j + 1 < NCH1)
            prep = {}

            def emit_av(info):
                h_, nk_, qlo_, vt_, ei_, rdy, is_last = info
                jb_lo = qlo_ // 128
                flags = []
                for c in range(2):
                    for jb in range(jb_lo, 4):
                        flags.append((c, jb, fbw[h_][jb]))
                        fbw[h_][jb] = False
                def f_av(e, nk_=nk_, vt_=vt_, ei_=ei_, flags=flags, h_=h_):
                    ins = None
                    for (c, jb, st_) in flags:
                        ins = e.matmul(accv[:, jb, c, :], lhsT=et[ei_][0:nk_, c, jb * 128:(jb + 1) * 128],
                                       rhs=VA[0:nk_, vt_, h_, :], start=st_, stop=False, skip_group_check=True)
                    return ins
                tav = P.op("pe", f_av, waits=[rdy, acc_free[0]])
                et_free[ei_] = tav
                if is_last:
                    tkp = head_post_evac(tav)
                    last_evac[0] = acc_free[0]
                    posts.append((h_, tkp))
                return tav

            for g, (h, idx, kt) in enumerate(stream):
                if idx == 0 and do_prep:
                    nxt = 4 * (j + 1) + h
                    ld = x_load(nxt)
                    prep[h] = (nxt, x_sumsq(nxt, ld, h))
                if kt == "m":
                    nk, qlo, vt = NMETA, 0, 32
                    klhs = lambda c, h=h: KTm[c * 64:(c + 1) * 64, h, :]
                else:
                    nk, vt = 128, kt
                    i_loc = kt - 4 * j
                    qlo = 128 * i_loc if i_loc >= 0 else 0
                    klhs = lambda c, kt=kt, h=h: KT[c * 64:(c + 1) * 64, h, kt * 128:(kt + 1) * 128]
                b = g % 2
                def f_qk(e, b=b, nk=nk, qlo=qlo, klhs=klhs, h=h):
                    ins = None
                    for c in range(2):
                        ins = e.matmul(psum[0:nk, 2 * b + c, qlo:CH1], lhsT=klhs(c), rhs=QT[c * 64:(c + 1) * 64, h, qlo:CH1],
                                       start=True, stop=True)
                    return ins
                tqk = P.op("pe", f_qk, waits=[st_free[b]])
                ei = eti[0] % 4
                eti[0] += 1
                texp = P.op("act", lambda e, b=b, nk=nk, qlo=qlo, ei=ei: e.activation(
                    out=et[ei][0:nk, :, qlo:CH1], in_=psum[0:nk, 2 * b:2 * b + 2, qlo:CH1], func=AF.Exp, scale=QK_SCALE),
                    waits=[tqk, et_free[ei]])
                st_free[b] = texp
                last_exp[0] = texp
                ready = texp
                if kt == "m":
                    if j == 0:
                        ready = P.op(mask_eng, lambda e, ei=ei, h=h: e.tensor_tensor(
                            out=et[ei][0:NMETA, :, 0:128], in0=et[ei][0:NMETA, :, 0:128], in1=MTm[:, 2 * h:2 * h + 2, :],
                            op=ALU.mult), waits=[texp, mt_tok])
                else:
                    i_loc = kt - 4 * j
                    if i_loc >= 0:
                        w = min(256, CH1 - qlo)
                        ready = P.op(mask_eng, lambda e, ei=ei, h=h, qlo=qlo, w=w: e.tensor_tensor(
                            out=et[ei][:, :, qlo:qlo + w], in0=et[ei][:, :, qlo:qlo + w], in1=MT[:, 2 * h:2 * h + 2, 0:w],
                            op=ALU.mult), waits=[texp, mt_tok])
                    elif i_loc == -1:
                        ready = P.op(mask_eng, lambda e, ei=ei, h=h: e.tensor_tensor(
                            out=et[ei][:, :, 0:128], in0=et[ei][:, :, 0:128], in1=MT[:, 2 * h:2 * h + 2, 128:256],
                            op=ALU.mult), waits=[texp, mt_tok])
                avq.append((h, nk, qlo, vt, ei, ready, idx == ntl - 1))
                if len(avq) > 2:
                    emit_av(avq.pop(0))
                if idx == min(7, ntl - 1):
                    flush_posts()
                    if do_prep:
                        nxt, sstok = prep[h]
                        tkr = x_rstd(nxt, sstok)
                        x_norm(nxt, h, [tkr])
            while avq:
                emit_av(avq.pop(0))
            for b_ in range(4):
                bank_free[b_] = last_exp[0]
                bank_free[4 + b_] = last_evac[0]
            ring_pos[0] = 0
        flush_posts()
        if not debug:
            kv_dead = [P.last["pe"]]
            for part_set in ((0, 1), (2,)):
                for dc in range(8):
                    for part in part_set:
                        if (dc, part) in W2_EARLY:
                            continue
                        wdma(w2toks, s_w2, lambda e, dc=dc, part=part: e.dma_start(
                            out=W2[:, dc, part * 1536:(part + 1) * 1536],
                            in_=w_in[dc * 128:(dc + 1) * 128, 1536 + part * 1536:1536 + (part + 1) * 1536]), extra=kv_dead)
                if part_set == (0, 1):
                    w2_lo = w2toks[-4:]
            w2_all = w2toks[-4:]
            wbtoks = []
            s_wb = [P.dsem(f"d_wb_{i}") for i in range(4)]
            for n_ in range(2):
                for fc in range(4):
                    wdma(wbtoks, s_wb, lambda e, n_=n_, fc=fc: e.dma_start(
                        out=WB[:, n_, fc, :], in_=wb[n_ * 512 + fc * 128:n_ * 512 + (fc + 1) * 128, :]), extra=kv_dead)
            wb_all = wbtoks[-4:]
            wdma(wotoks, s_wo, lambda e: e.dma_start(out=WO[:, 0, :], in_=wo[0:128, :]), extra=kv_dead)
            wo_all = wotoks[-4:]

        P.barrier()
        at_transposes(S - CH1)

        P.barrier()
        if debug:
            s_dbg = P.dsem("d_dbg")
            dt_ = []
            def ddma(o, i):
                dt_.append(P.dma("pool", lambda e, o=o, i=i: e.dma_start(out=o, in_=i), s_dbg))
                P.wait_only("pool", [dt_[-1]])
            for h_ in range(4):
                for q_ in range(4):
                    ddma(dbg_kt[:, h_ * S + q_ * 1024:h_ * S + (q_ + 1) * 1024], KT[:, h_, q_ * 1024:(q_ + 1) * 1024])
                    ddma(dbg_at[:, h_ * S + q_ * 1024:h_ * S + (q_ + 1) * 1024], AT[:, h_, q_ * 1024:(q_ + 1) * 1024])
                ddma(dbg_qt[:, h_ * CH1:(h_ + 1) * CH1], QT[:, h_, :])
            for t_ in range(33):
                ddma(dbg_va[:, t_ * 516:(t_ + 1) * 516], VA[:, t_, :, :].rearrange("p h n -> p (h n)"))
            ddma(dbg_mt[:, :], MT[:, :, :].rearrange("p a b -> p (a b)"))
            ddma(dbg_mtm[:, :], MTm[:, :, :].rearrange("p a b -> p (a b)"))
            ddma(dbg_sm[:, 0:40], rstd_all)
            ddma(dbg_sm[:, 40:48], lam_s)
            ddma(dbg_sm[:, 48:50], gqk)
            ddma(dbg_ktm[:, :], KTm[:, :, :].rearrange("p a b -> p (a b)"))

        if debug:
            kv_dead = []
            for part_set in ((0, 1), (2,)):
                for dc in range(8):
                    for part in part_set:
                        if (dc, part) in W2_EARLY:
                            continue
                        wdma(w2toks, s_w2, lambda e, dc=dc, part=part: e.dma_start(
                            out=W2[:, dc, part * 1536:(part + 1) * 1536],
                            in_=w_in[dc * 128:(dc + 1) * 128, 1536 + part * 1536:1536 + (part + 1) * 1536]), extra=kv_dead)
                if part_set == (0, 1):
                    w2_lo = w2toks[-4:]
            w2_all = w2toks[-4:]
            wbtoks = []
            s_wb = [P.dsem(f"d_wb_{i}") for i in range(4)]
            for n_ in range(2):
                for fc in range(4):
                    wdma(wbtoks, s_wb, lambda e, n_=n_, fc=fc: e.dma_start(
                        out=WB[:, n_, fc, :], in_=wb[n_ * 512 + fc * 128:n_ * 512 + (fc + 1) * 128, :]), extra=kv_dead)
            wb_all = wbtoks[-4:]
            wdma(wotoks, s_wo, lambda e: e.dma_start(out=WO[:, 0, :], in_=wo[0:128, :]), extra=kv_dead)
            wo_all = wotoks[-4:]

        NT2 = CH2 // 128
        NCH2 = S // CH2

        def x_prep2(n, xslot):
            slot = n % 2
            ld = P.dma("sp", lambda e, n=n, slot=slot: e.dma_start(out=xr[:, slot, :], in_=x[n * 128:(n + 1) * 128, :]),
                       s_xr[slot], waits=[xr_free[slot]])
            x_norm(n, xslot, [ld, xn_free[0]])

        def proj2(woff, src, n):
            bi, bw = bank_acquire()
            def f(e, bi=bi, woff=woff, src=src, n=n):
                ins = None
                for dc in range(8):
                    ins = e.matmul(psum[:, bi, 0:n], lhsT=W2[:, dc, woff:woff + 128], rhs=src[:, dc, 0:n],
                                   start=(dc == 0), stop=(dc == 7))
                return ins
            tp = P.op("pe", f, waits=[bw] + (w2_lo if woff + 128 <= 3072 else w2_all) + xnT_tok[0])
            return bi, tp

        carry_tok = [None] * 4
        xnT2 = xnT[:, :, 0:CH2]
        for fc in range(4):
            bcc, tcc = proj2(1024 + fc * 128, xnTm, NMETA)
            bch, tch = proj2(1536 + fc * 128, xnTm, NMETA)
            tk = P.op("act", lambda e, bch=bch, fc=fc: e.activation(out=chm4[:, fc, :], in_=psum[:, bch, 0:NMETA], func=AF.Copy), waits=[tch])
            bank_free[bch] = tk
            tk = P.op("dve", lambda e, bcc=bcc, fc=fc: e.tensor_tensor(out=um4[:, fc, :], in0=psum[:, bcc, 0:NMETA], in1=chm4[:, fc, :], op=ALU.mult),
                      waits=[tk, tcc])
            bank_free[bcc] = tk
            carry_tok[fc] = P.op("dve", lambda e, fc=fc: e.tensor_copy(out=carry[:, fc, :], in_=um4[:, fc, NMETA - 2:NMETA]), after_prev=True)

        if debug:
            ddma2 = P.dma("pool", lambda e: e.dma_start(out=dbg_carry[:, :], in_=carry[:, :, :].rearrange("p a b -> p (a b)")), s_dbg,
                          waits=[carry_tok[3]])
            P.wait_only("pool", [ddma2])
            ddma2 = P.dma("pool", lambda e: e.dma_start(out=dbg_xntm[:, :], in_=xnTm[:, :, :].rearrange("p a b -> p (a b)")), s_dbg)
            P.wait_only("pool", [ddma2])
        for s_ in range(NT2):
            x_prep2(s_, s_)

        ob_store = [None, None]
        ci = [0]
        gi = [0]
        do_transposes(NT2, 0, xnT2)
        if NCH2 > 1:
            for s_ in range(NT2):
                x_prep2(NT2 + s_, s_)
        for j in range(NCH2):
            tok0 = j * CH2
            for fc in range(4):
                k = ci[0] % 2
                ci[0] += 1
                bgc, tgc = proj2(2048 + fc * 128, xnT2, CH2)
                bch, tch = proj2(1536 + fc * 128, xnT2, CH2)
                bcc, tcc = proj2(1024 + fc * 128, xnT2, CH2)
                bcb, tcb = proj2(512 + fc * 128, xnT2, CH2)
                tsg = P.op("act", lambda e, bgc=bgc, k=k: e.activation(out=sg[k], in_=psum[:, bgc, 0:CH2], func=AF.Sigmoid), waits=[tgc, rfree.get(("sg", k))])
                tchc = P.op("act", lambda e, bch=bch, k=k: e.activation(out=chb[k], in_=psum[:, bch, 0:CH2], func=AF.Copy), waits=[tch, rfree.get(("chb", k))])
                bank_free[bch] = tchc
                tcbs = P.op("act", lambda e, bcb=bcb, k=k: e.activation(out=cbs[k], in_=psum[:, bcb, 0:CH2], func=AF.Copy),
                            waits=[tcb, rfree.get(("cbs", k))])
                bank_free[bcb] = tcbs
                tsl = P.op("dve", lambda e, bgc=bgc, k=k: e.tensor_tensor(out=sg[k], in0=psum[:, bgc, 0:CH2], in1=sg[k], op=ALU.mult),
                           waits=[tsg])
                bank_free[bgc] = tsl
                P.op("dve", lambda e, fc=fc, k=k: e.tensor_copy(out=ubuf[k][:, 0:2], in_=carry[:, fc, :]), waits=[carry_tok[fc]])
                tu = P.op("dve", lambda e, bcc=bcc, k=k: e.tensor_tensor(out=ubuf[k][:, 2:CH2 + 2], in0=psum[:, bcc, 0:CH2], in1=chb[k],
                                                                       op=ALU.mult), waits=[tchc, tcc])
                bank_free[bcc] = tu
                rfree[("chb", k)] = tu
                carry_tok[fc] = P.op("dve", lambda e, fc=fc, k=k: e.tensor_copy(out=carry[:, fc, :], in_=ubuf[k][:, CH2:CH2 + 2]),
                                     after_prev=True)
                P.op("dve", lambda e, fc=fc, k=k: e.tensor_scalar(out=yb[k], in0=ubuf[k][:, 0:CH2], scalar1=cw[:, 0, fc:fc + 1],
                                                                 scalar2=None, op0=ALU.mult), waits=[rfree.get(("yb", k))], after_prev=True)
                P.op("dve", lambda e, fc=fc, k=k: e.scalar_tensor_tensor(out=yb[k], in0=ubuf[k][:, 1:CH2 + 1], scalar=cw[:, 1, fc:fc + 1],
                                                                        in1=yb[k], op0=ALU.mult, op1=ALU.add), after_prev=True)
                P.op("dve", lambda e, fc=fc, k=k: e.scalar_tensor_tensor(out=yb[k], in0=ubuf[k][:, 2:CH2 + 2], scalar=cw[:, 2, fc:fc + 1],
                                                                        in1=yb[k], op0=ALU.mult, op1=ALU.add), after_prev=True)
                ty = P.op("dve", lambda e, k=k: e.tensor_tensor(out=yb[k], in0=cbs[k], in1=yb[k], op=ALU.mult),
                          waits=[tcbs], after_prev=True)
                rfree[("cbs", k)] = ty
                tpl = P.op("pool", lambda e, fc=fc, k=k: e.tensor_tensor(out=cT[:, fc, :], in0=yb[k], in1=sg[k], op=ALU.mult), waits=[ty])
                rfree[("sg", k)] = tpl
                rfree[("yb", k)] = tpl
            for h in range(4):
                k = ci[0] % 2
                ci[0] += 1
                bga, tga = proj2(h * 128, xnT2, CH2)
                ts_ = P.op("act", lambda e, bga=bga, k=k: e.activation(out=sg2[k], in_=psum[:, bga, 0:CH2], func=AF.Sigmoid), waits=[tga, rfree.get(("sg2", k))])
                tsl = P.op("dve", lambda e, bga=bga, k=k: e.tensor_tensor(out=sg2[k], in0=psum[:, bga, 0:CH2], in1=sg2[k], op=ALU.mult),
                           waits=[ts_])
                bank_free[bga] = tsl
                tpl = P.op("pool", lambda e, h=h, k=k, tok0=tok0: e.tensor_tensor(out=aTg[:, h, :], in0=AT[:, h, tok0:tok0 + CH2], in1=sg2[k],
                                                                                 op=ALU.mult), waits=[tsl])
                rfree[("sg2", k)] = tpl
            br_ready = P.last["pool"]
            pgs = {}
            def emit_pg(m):
                kk = gi[0] % 5
                gi[0] += 1
                bg0, tg0 = proj2(2560 + m * 128, xnT2, CH2)
                bg1, tg1 = proj2(3584 + m * 128, xnT2, CH2)
                ta0 = P.op("act", lambda e, bg0=bg0, kk=kk: e.activation(out=g0b[kk], in_=psum[:, bg0, 0:CH2], func=AF.Sigmoid),
                           waits=[tg0, rfree.get(("g01", kk))])
                bank_free[bg0] = ta0
                ta1 = P.op("act", lambda e, bg1=bg1, kk=kk: e.activation(out=g1b[kk], in_=psum[:, bg1, 0:CH2], func=AF.Sigmoid), waits=[tg1])
                bank_free[bg1] = ta1
                pgs[m] = (kk, ta0, ta1)
            for m in range(4):
                emit_pg(m)
            mT_tok = [None] * 8
            for m in range(8):
                kk, ta0, ta1 = pgs.pop(m)
                by0, bw0 = bank_acquire()
                by1, bw1 = bank_acquire()
                def f_y(e, by=by0, src=aTg, n_=0, m=m):
                    ins = None
                    for fc in range(4):
                        ins = e.matmul(psum[:, by, 0:CH2], lhsT=WB[:, n_, fc, m * 128:(m + 1) * 128], rhs=src[:, fc, :],
                                       start=(fc == 0), stop=(fc == 3))
                    return ins
                ty0 = P.op("pe", f_y, waits=[bw0, br_ready] + wb_all)
                def f_y1(e, by=by1, src=cT, n_=1, m=m):
                    ins = None
                    for fc in range(4):
                        ins = e.matmul(psum[:, by, 0:CH2], lhsT=WB[:, n_, fc, m * 128:(m + 1) * 128], rhs=src[:, fc, :],
                                       start=(fc == 0), stop=(fc == 3))
                    return ins
                ty1 = P.op("pe", f_y1, waits=[bw1])
                if m + 4 < 8:
                    emit_pg(m + 4)
                if m == 3 and j + 1 < NCH2:
                    do_transposes(NT2, NT2 * (j + 1), xnT2)
                    if j + 2 < NCH2:
                        for s_ in range(NT2):
                            x_prep2(NT2 * (j + 2) + s_, s_)
                td0 = P.op("dve", lambda e, by0=by0, kk=kk: e.tensor_tensor(out=g0b[kk], in0=psum[:, by0, 0:CH2], in1=g0b[kk], op=ALU.mult),
                           waits=[ta0, ty0])
                bank_free[by0] = td0
                td1 = P.op("dve", lambda e, by1=by1, kk=kk: e.tensor_tensor(out=g1b[kk], in0=psum[:, by1, 0:CH2], in1=g1b[kk], op=ALU.mult),
                           waits=[ta1, ty1])
                bank_free[by1] = td1
                tpl = P.op("pool", lambda e, m=m, kk=kk: e.tensor_tensor(out=mT[:, m, :], in0=g0b[kk], in1=g1b[kk], op=ALU.add), waits=[td1])
                rfree[("g01", kk)] = tpl
                mT_tok[m] = tpl
            groups = []
            lds = {}
            for s_ in range(NT2):
                n = NT2 * j + s_
                slot = n % 2
                lds[s_] = P.dma("sp", lambda e, n=n, slot=slot: e.dma_start(out=ob[slot], in_=x[n * 128:(n + 1) * 128, :]),
                                s_obl[slot], waits=[ob_store[slot]])
                for nh in range(2):
                    bi, bw = bank_acquire()
                    groups.append((s_, nh, bi, bw))
            tp = None
            for m in range(8):
                def f_o(e, m=m, groups=tuple(groups)):
                    ins = None
                    for (s_, nh, bi, bw) in groups:
                        ins = e.matmul(psum[:, bi, :], lhsT=mT[:, m, s_ * 128:(s_ + 1) * 128], rhs=WO[:, m, nh * 512:(nh + 1) * 512],
                                       start=(m == 0), stop=(m == 7))
                    return ins
                w_ = [mT_tok[m]]
                if m == 0:
                    w_ += [g[3] for g in groups] + wo_all
                tp = P.op("pe", f_o, waits=w_)
            for s_ in range(NT2):
                n = NT2 * j + s_
                slot = n % 2
                tl = None
                for (s2, nh, bi, bw) in groups:
                    if s2 != s_:
                        continue
                    tl = P.op("dve", lambda e, bi=bi, slot=slot, nh=nh: e.tensor_tensor(
                        out=ob[slot][:, nh * 512:(nh + 1) * 512], in0=psum[:, bi, :], in1=ob[slot][:, nh * 512:(nh + 1) * 512],
                        op=ALU.add), waits=[tp, lds[s_]])
                    bank_free[bi] = tl
                ob_store[slot] = P.dma("sp", lambda e, n=n, slot=slot: e.dma_start(out=out[n * 128:(n + 1) * 128, :], in_=ob[slot]),
                                       s_obs[slot], waits=[tl])
        P.wait_only("sp", [ob_store[0], ob_store[1]])

        with nc.Block() as block:
            @block.tensor
            def _(e):
                P.replay("pe", e)

            @block.scalar
            def _(e):
                P.replay("act", e)

            @block.vector
            def _(e):
                P.replay("dve", e)

            @block.gpsimd
            def _(e):
                P.replay("pool", e)

            @block.sync
            def _(e):
                P.replay("sp", e)
    return nc


_CACHE = {}


def _consts():
    d = np.arange(NG)
    b = _rel_bucket(d)
    ohm = np.zeros((32, NG), np.float32)
    ohm[b, d] = 1.0
    ident = np.eye(128, dtype=np.float32)
    bones = np.zeros((128, 128), np.float32)
    bones[:64, :64] = 1.0
    bones[64:, 64:] = 1.0
    return ohm, ident, bones


def run_debug(inputs, ncores=1):
    ins = dict(inputs)
    f = lambda a: np.ascontiguousarray(np.asarray(a, dtype=np.float32))
    x = f(ins["x"])
    ohm, ident, bones = _consts()
    shared = {
        "meta": f(ins["meta_tokens"]), "relb": f(ins["rel_bias"]).reshape(32, 8), "norm_g": f(ins["norm_g"]).reshape(1, D),
        "w_in": f(ins["w_in"]).reshape(D, 6144), "qg": f(ins["q_norm_g"]).reshape(1, 64), "kg": f(ins["k_norm_g"]).reshape(1, 64),
        "lq1": f(ins["lambda_q1"]).reshape(1, 64), "lk1": f(ins["lambda_k1"]).reshape(1, 64),
        "lq2": f(ins["lambda_q2"]).reshape(1, 64), "lk2": f(ins["lambda_k2"]).reshape(1, 64),
        "subg": f(ins["subln_g"]).reshape(1, 128), "convw": f(ins["conv_w"]).reshape(3, 512),
        "wb": f(ins["w_branch"]).reshape(1024, 1024), "wo": f(ins["w_out"]).reshape(1024, 1024),
        "oh": ohm, "ident": ident, "bones": bones,
    }
    nc = build_program(debug=True)
    in_maps = [dict(shared, x=x[b]) for b in range(ncores)]
    res = run_bass_kernel_spmd(nc, in_maps, core_ids=list(range(ncores)))
    return res.results


def kernel(x, meta_tokens, rel_bias, norm_g, w_in, q_norm_g, k_norm_g, lambda_q1, lambda_k1,
           lambda_q2, lambda_k2, subln_g, conv_w, w_branch, w_out):
    f = lambda a: np.ascontiguousarray(np.asarray(a, dtype=np.float32))
    x = f(x)
    ohm, ident, bones = _consts()
    shared = {
        "meta": f(meta_tokens), "relb": f(rel_bias).reshape(32, 8), "norm_g": f(norm_g).reshape(1, D),
        "w_in": f(w_in).reshape(D, 6144), "qg": f(q_norm_g).reshape(1, 64), "kg": f(k_norm_g).reshape(1, 64),
        "lq1": f(lambda_q1).reshape(1, 64), "lk1": f(lambda_k1).reshape(1, 64),
        "lq2": f(lambda_q2).reshape(1, 64), "lk2": f(lambda_k2).reshape(1, 64),
        "subg": f(subln_g).reshape(1, 128), "convw": f(conv_w).reshape(3, 512),
        "wb": f(w_branch).reshape(1024, 1024), "wo": f(w_out).reshape(1024, 1024),
        "oh": ohm, "ident": ident, "bones": bones,
    }
    if "nc" not in _CACHE:
        _CACHE["nc"] = build_program()
    nc = _CACHE["nc"]
    in_maps = [dict(shared, x=x[b]) for b in range(NCORES)]
    res = run_bass_kernel_spmd(nc, in_maps, core_ids=list(range(NCORES)))
    return np.stack([np.asarray(r["out"], dtype=np.float32) for r in res.results], axis=0)
```
